# Optimizing a Trainium2 kernel written in Bass

```python
import jax, jax.numpy as jnp
from jax import lax
import numpy as np

D_MODEL = 1024
BATCH = 8
SEQ = 2048
DEPTH = 2

D_MIX = D_MODEL
D_POOL = D_MIX // 2
POOL_WINDOWS = (2, 4, 8, 16)
N_POOL_GROUPS = len(POOL_WINDOWS)
POOL_GROUP = D_POOL // N_POOL_GROUPS
HEAD_DIM = 64
D_ATTN = D_MIX - D_POOL
N_HEADS = D_ATTN // HEAD_DIM
N_KV_HEADS = 2
GQA_GROUP = N_HEADS // N_KV_HEADS
D_KV = N_KV_HEADS * HEAD_DIM
WINDOW = 128
BLOCK = 128
IN_WIDTHS = (D_POOL, D_POOL, D_ATTN, D_KV, D_KV, D_ATTN)
D_IN = sum(IN_WIDTHS)
EPS = 1e-6
NEG_INF = -1e30

kernel_name = "hybrid_pool_swa_sink_parallel_heads"


def rmsnorm(x, gain):
    x32 = x.astype(jnp.float32)
    y = x32 * lax.rsqrt(jnp.mean(x32 * x32, axis=-1, keepdims=True) + EPS) * gain.astype(jnp.float32)
    return y.astype(x.dtype)


def alibi_slopes():
    return jnp.exp2(-8.0 * jnp.arange(1, N_HEADS + 1, dtype=jnp.float32) / N_HEADS)


def pool_mixer(u, w_grp, scale):
    B, S, _ = u.shape
    u32 = u.astype(jnp.float32).reshape(B, S, N_POOL_GROUPS, POOL_GROUP)
    csum = jnp.cumsum(u32, axis=1)
    csum = jnp.concatenate([jnp.zeros_like(csum[:, :1]), csum], axis=1)
    pos = jnp.arange(1, S + 1, dtype=jnp.float32)
    means = []
    for g, w in enumerate(POOL_WINDOWS):
        c = csum[:, :, g]
        lo = jnp.concatenate([jnp.zeros_like(c[:, :w - 1]), c[:, :S + 1 - w]], axis=1)
        count = jnp.minimum(pos, float(w))[None, :, None]
        means.append((c[:, 1:] - lo) / count)
    pooled = jnp.stack(means, axis=2) - u32
    mixed = jnp.einsum('bsgc,gcd->bsgd', pooled.astype(u.dtype), w_grp)
    return mixed.reshape(B, S, D_POOL) * scale


def swa_sink_attention(q, k, v, sinks):
    B, S, _ = q.shape
    NB = S // BLOCK
    q = q.reshape(B, NB, BLOCK, N_KV_HEADS, GQA_GROUP, HEAD_DIM)
    k = k.reshape(B, NB, BLOCK, N_KV_HEADS, HEAD_DIM)
    v = v.reshape(B, NB, BLOCK, N_KV_HEADS, HEAD_DIM)

    def with_prev(t):
        prev = jnp.concatenate([jnp.zeros_like(t[:, :1]), t[:, :-1]], axis=1)
        return jnp.concatenate([prev, t], axis=2)

    kb, vb = with_prev(k), with_prev(v)
    scores = jnp.einsum('bnqhgd,bnkhd->bnhgqk', q, kb).astype(jnp.float32) * (HEAD_DIM ** -0.5)
    qi = jnp.arange(BLOCK)[:, None]
    kj = jnp.arange(2 * BLOCK)[None, :]
    dist = qi + BLOCK - kj
    in_win = (dist >= 0) & (dist < WINDOW)
    key_exists = (jnp.arange(NB)[:, None, None] > 0) | (kj >= BLOCK)[None]
    valid = in_win[None] & key_exists
    slopes = alibi_slopes().reshape(N_KV_HEADS, GQA_GROUP)
    bias = -slopes[:, :, None, None] * dist.astype(jnp.float32)
    scores = jnp.where(valid[None, :, None, None], scores + bias, NEG_INF)
    sink = jnp.broadcast_to(sinks.astype(jnp.float32).reshape(N_KV_HEADS, GQA_GROUP, 1, 1),
                            scores.shape[:-1] + (1,))
    probs = jax.nn.softmax(jnp.concatenate([scores, sink], axis=-1), axis=-1)[..., :-1]
    out = jnp.einsum('bnhgqk,bnkhd->bnqhgd', probs.astype(v.dtype), vb)
    return out.reshape(B, S, D_ATTN)


def setup_inputs(seed: int = 0) -> dict:
    key = jax.random.key(seed)
    ks = jax.random.split(key, 9)
    x = jax.random.normal(ks[0], (BATCH, SEQ, D_MODEL), jnp.float32)
    w_in = jax.random.normal(ks[1], (DEPTH, D_MODEL, D_IN), jnp.float32) * D_MODEL ** -0.5
    pool_w = jax.random.normal(ks[2], (DEPTH, N_POOL_GROUPS, POOL_GROUP, POOL_GROUP), jnp.float32) * POOL_GROUP ** -0.5
    pool_scale = 1.0 + 0.1 * jax.random.normal(ks[3], (DEPTH, D_POOL), jnp.float32)
    attn_sinks = 0.5 * jax.random.normal(ks[4], (DEPTH, N_HEADS), jnp.float32)
    w_out = jax.random.normal(ks[5], (DEPTH, D_MIX, D_MODEL), jnp.float32) * D_MIX ** -0.5
    norm_pre = 1.0 + 0.1 * jax.random.normal(ks[6], (DEPTH, D_MODEL), jnp.float32)
    norm_post = 1.0 + 0.1 * jax.random.normal(ks[7], (DEPTH, D_MODEL), jnp.float32)
    return {"x": x, "w_in": w_in, "pool_w": pool_w, "pool_scale": pool_scale,
            "attn_sinks": attn_sinks, "w_out": w_out, "norm_pre": norm_pre, "norm_post": norm_post}


def reference(x, w_in, pool_w, pool_scale, attn_sinks, w_out, norm_pre, norm_post):
    splits = [int(s) for s in np.cumsum(IN_WIDTHS)[:-1]]
    for layer in range(DEPTH):
        h = rmsnorm(x, norm_pre[layer])
        proj = h @ w_in[layer]
        pool_u, pool_gate, q, k, v, attn_gate = jnp.split(proj, splits, axis=-1)
        pool_out = pool_mixer(pool_u, pool_w[layer], pool_scale[layer]) * jax.nn.silu(pool_gate)
        attn_out = swa_sink_attention(q, k, v, attn_sinks[layer]) * jax.nn.silu(attn_gate)
        y = jnp.concatenate([pool_out, attn_out], axis=-1) @ w_out[layer]
        x = x + rmsnorm(y, norm_post[layer])
    return x
```

```python
import contextlib
import numpy as np
import concourse.bass as bass
import concourse.mybir as mybir
from concourse.bass_utils import run_bass_kernel_spmd

F32 = mybir.dt.float32
BF16 = mybir.dt.bfloat16
AF = mybir.ActivationFunctionType
ALU = mybir.AluOpType

S = 2048
D = 1024
NT = 16
NG = 4
DEPTH = 2
D_IN = 2304
C_U, C_PG, C_Q, C_K, C_V, C_AG = 0, 512, 1024, 1536, 1664, 1792
MASK = -30000.0
EPS = 1e-6

COMPUTE = ("pe", "act", "dve", "pool")
QUEUES = ("pe", "act", "dve", "pool", "sp")


class Op:
    __slots__ = ("idx", "q", "fn", "reads", "writes", "is_dma", "group", "deps",
                 "eidx", "signal", "count", "name")


class Prog:
    def __init__(self, nc):
        self.nc = nc
        self.ops = []
        self.groups = {}

    def op(self, q, fn, reads=(), writes=(), name=""):
        o = Op()
        o.idx = len(self.ops); o.q = q; o.fn = fn
        o.reads = tuple(reads); o.writes = tuple(writes)
        o.is_dma = False; o.group = None; o.name = name
        self.ops.append(o)
        return o

    def dma(self, q, fn, reads=(), writes=(), group=None, name=""):
        o = self.op(q, fn, reads, writes, name)
        o.is_dma = True
        if group is None:
            group = "d%d" % o.idx
        o.group = group
        self.groups.setdefault(group, []).append(o.idx)
        return o

    def build(self):
        ops = self.ops
        last_w = {}
        readers = {}
        eng_n = {q: 0 for q in QUEUES}
        for o in ops:
            o.eidx = eng_n[o.q]; eng_n[o.q] += 1
            raw = set(); oth = set()
            for r in o.reads:
                if r in last_w:
                    raw.add(last_w[r])
            for w in o.writes:
                if w in last_w:
                    oth.add(last_w[w])
                for rd in readers.get(w, ()):
                    oth.add(rd)
            oth.discard(o.idx)
            o.deps = (raw, oth - raw)
            for r in o.reads:
                readers.setdefault(r, []).append(o.idx)
            for w in o.writes:
                last_w[w] = o.idx
                readers[w] = []
        waited = {q: {e: -1 for e in COMPUTE} for q in QUEUES}
        waited_grp = {q: set() for q in QUEUES}
        for o in ops:
            o.signal = False
        plan = []
        for o in ops:
            raw, oth = o.deps
            need_eng = {}
            need_grp = set()
            for d_idx, is_raw in [(d, True) for d in raw] + [(d, False) for d in oth]:
                d = ops[d_idx]
                if d.is_dma:
                    if d.group not in waited_grp[o.q]:
                        need_grp.add(d.group)
                    continue
                if d.q == o.q and not o.is_dma:
                    if o.q == "pe":
                        continue
                if d.eidx <= waited[o.q][d.q]:
                    continue
                if d.q not in need_eng or ops[need_eng[d.q]].eidx < d.eidx:
                    need_eng[d.q] = d_idx
            w = []
            for e, d_idx in need_eng.items():
                ops[d_idx].signal = True
                waited[o.q][e] = ops[d_idx].eidx
                w.append(("eng", e, d_idx))
            for g in sorted(need_grp):
                waited_grp[o.q].add(g)
                w.append(("grp", g))
            plan.append(w)
        cnt = {q: 0 for q in COMPUTE}
        for o in ops:
            if o.is_dma:
                continue
            if o.signal:
                cnt[o.q] += 1
                o.count = cnt[o.q]
        self.n_signals = dict(cnt)
        return plan

    def emit(self, final_wait_groups=()):
        nc = self.nc
        plan = self.build()
        ops = self.ops
        with contextlib.ExitStack() as es:
            esem = {q: es.enter_context(nc.semaphore("s_" + q)) for q in COMPUTE}
            gsem = {g: es.enter_context(nc.semaphore("g_" + g)) for g in self.groups}
            block = es.enter_context(nc.Block())

            def run(q, eng):
                for o in ops:
                    if o.q != q:
                        continue
                    for w in plan[o.idx]:
                        if w[0] == "eng":
                            eng.wait_ge(esem[w[1]], ops[w[2]].count)
                        else:
                            eng.wait_ge(gsem[w[1]], 16 * len(self.groups[w[1]]))
                    ins = o.fn(eng)
                    if o.is_dma:
                        ins.then_inc(gsem[o.group], 16)
                    elif o.signal:
                        ins.then_inc(esem[o.q], 1)
                if q == "sp":
                    for g in final_wait_groups:
                        eng.wait_ge(gsem[g], 16 * len(self.groups[g]))

            @block.sync
            def _(e):
                run("sp", e)

            @block.tensor
            def _(e):
                run("pe", e)

            @block.scalar
            def _(e):
                run("act", e)

            @block.vector
            def _(e):
                run("dve", e)

            @block.gpsimd
            def _(e):
                run("pool", e)


_HSLOT = {0: 0, 2: 1, 1: 2, 3: 3, 4: 4, 6: 5, 5: 6, 7: 7}


def _const_tables():
    slopes = 2.0 ** (-np.arange(1, 9, dtype=np.float64))
    p = np.arange(128)[:, None].astype(np.float64)
    i = np.arange(128)[None, :].astype(np.float64)
    bias = np.zeros((128, 8, 256), np.float32)
    for h in range(8):
        off = np.where(i < p, np.exp(-slopes[h] * (i + 128 - p)), 0.0)
        dg = np.where(i >= p, np.exp(-slopes[h] * (i - p)), 0.0)
        hs = _HSLOT[h]
        bias[:, hs, 0:128] = off
        bias[:, hs, 128:256] = dg
    at = np.zeros((128, 12, 128), np.float32)
    s = np.arange(128)[:, None].astype(np.float64)
    t = np.arange(128)[None, :].astype(np.float64)
    eye = (s == t).astype(np.float64)
    for g, w in enumerate((2, 4, 8, 16)):
        band = ((t - s >= 0) & (t - s < w)).astype(np.float64)
        at[:, 3 * g + 0, :] = band / w - eye
        at[:, 3 * g + 1, :] = ((t + 128 - s) < w).astype(np.float64) / w
        at[:, 3 * g + 2, :] = band / np.minimum(t + 1, w) - eye
    return bias, at


def build_program(n_layers=DEPTH, n_groups=NG, dbg=False):
    nc = bass.Bass("TRN2", target_bir_lowering=False)
    x_d = nc.dram_tensor("x", [S, D], F32, kind="ExternalInput").ap()
    win_d = nc.dram_tensor("w_in", [DEPTH, D, D_IN], F32, kind="ExternalInput").ap()
    pw_d = nc.dram_tensor("pool_w", [DEPTH, 4, 128, 128], F32, kind="ExternalInput").ap()
    psc_d = nc.dram_tensor("pool_scale", [DEPTH, 512], F32, kind="ExternalInput").ap()
    sink_d = nc.dram_tensor("attn_sinks", [DEPTH, 8], F32, kind="ExternalInput").ap()
    wout_d = nc.dram_tensor("w_out", [DEPTH, D, D], F32, kind="ExternalInput").ap()
    gpre_d = nc.dram_tensor("norm_pre", [DEPTH, D], F32, kind="ExternalInput").ap()
    gpost_d = nc.dram_tensor("norm_post", [DEPTH, D], F32, kind="ExternalInput").ap()
    bias_d = nc.dram_tensor("c_bias", [128, 8, 256], F32, kind="ExternalInput").ap()
    at_d = nc.dram_tensor("c_at", [128, 12, 128], F32, kind="ExternalInput").ap()
    out_d = nc.dram_tensor("out", [S, D], F32, kind="ExternalOutput").ap()
    x_t = x_d.rearrange("(n p) d -> n p d", p=128)
    out_t = out_d.rearrange("(n p) d -> n p d", p=128)

    with contextlib.ExitStack() as es:
        def sb(name, shape, dt):
            return es.enter_context(nc.sbuf_tensor(name, shape, dt))

        X = sb("X", [128, NT, D], F32)
        WIN = sb("WIN", [128, 8, D_IN], BF16)
        WOUT = sb("WOUT", [128, 8, D], BF16)
        PW = sb("PW", [128, DEPTH, 4, 128], BF16)
        GPREC = sb("GPREC", [128, DEPTH, 8], F32)
        GPOST = sb("GPOST", [128, D], F32)
        PSC = sb("PSC", [128, DEPTH, 4], F32)
        ESINK = sb("ESINK", [128, DEPTH, 8], F32)
        IDENT = sb("IDENT", [128, 128], BF16)
        BIAS = sb("BIAS", [128, 8, 256], BF16)
        AT = sb("AT", [128, 12, 128], BF16)
        NR = 12
        XS = sb("XS", [128, 4, D], BF16)
        HT = sb("HT", [128, 1, 8, 512], BF16)
        QT = sb("QT", [128, 4, 512], BF16)
        KT = sb("KT", [128, 2, NR, 128], BF16)
        VA = sb("VA", [128, NR, 2, 65], BF16)
        NU = 9
        U = sb("U", [128, NU, 512], BF16)
        SGP = sb("SGP", [128, 4, 512], BF16)
        SGA = sb("SGA", [128, 4, 512], BF16)
        PT = sb("PT", [128, 2, 8, 256], BF16)
        AO = sb("AO", [128, 1, 512], BF16)
        PVS = sb("PVS", [128, 2, 2, 260], F32)
        PLT = sb("PLT", [128, 2, 512], BF16)
        CT = sb("CT", [128, 8, 512], BF16)
        T = sb("T", [128, 2, D], F32)
        SSQ = sb("SSQ", [128, DEPTH, NT], F32)
        RSTD = sb("RSTD", [128, DEPTH, NT], F32)
        SSQY = sb("SSQY", [128, DEPTH, 2 * NT], F32)
        RSTDY = sb("RSTDY", [128, DEPTH, NT], F32)
        DEN = sb("DEN", [128, 2, 8], F32)
        RDEN = sb("RDEN", [128, 2, 8], F32)
        EPSB = sb("EPSB", [128, 1], F32)
        ONEB = sb("ONEB", [128, 1], F32)

        banks = [es.enter_context(nc.psum_tensor("bank%d" % b, [128, 512], F32)) for b in range(8)]

        def bank_f32(b):
            return banks[b][:, :]

        def bank_bf16(b):
            return banks[b][:, :].bitcast(BF16)

        P = Prog(nc)
        state = {"gen": 0, "gate": 0, "sc": 0, "tp": 0, "banks": [1, 2, 3]}

        def alloc_gen():
            pool_ = state["banks"]
            b = pool_[state["gen"] % len(pool_)]
            state["gen"] += 1
            return b

        def BK(b):
            return ("BK", b)

        P.op("pool", lambda e: e.memset(EPSB[:, :], EPS), writes=["EPSB"])
        P.op("pool", lambda e: e.memset(ONEB[:, :], 1.0), writes=["ONEB"])
        P.op("pool", lambda e: e.memset(SSQ[:, :, :], 0.0), writes=[("SSQ", l_, i_) for l_ in range(DEPTH) for i_ in range(NT)])
        P.op("pool", lambda e: e.memset(SSQY[:, :, :], 0.0), writes=[("SSQY", l_, i_, h_) for l_ in range(DEPTH) for i_ in range(NT) for h_ in range(2)])
        P.op("pool", lambda e: e.memset(T[:, 0, 0:128], 0.0), writes=[("T", 0, 0)])
        P.op("pool", lambda e: e.affine_select(T[:, 0, 0:128], T[:, 0, 0:128], pattern=[[-1, 128]],
                                               compare_op=ALU.not_equal, fill=1.0, base=0,
                                               channel_multiplier=1), reads=[("T", 0, 0)], writes=[("T", 0, 0)])
        P.op("pool", lambda e: e.tensor_copy(IDENT[:, :], T[:, 0, 0:128]), reads=[("T", 0, 0)], writes=["IDENT"])
        P.op("pool", lambda e: e.memset(VA[:, :, :, 64:65], 1.0), writes=["VA1"])

        def ld_gprec(e):
            with nc.allow_non_contiguous_dma(reason="tiny strided parameter loads"):
                return e.dma_start(out=GPREC[:, :, :], in_=gpre_d.rearrange("l (k p) -> p l k", p=128))
        def ld_psc(e):
            with nc.allow_non_contiguous_dma(reason="tiny strided parameter loads"):
                return e.dma_start(out=PSC[:, :, :], in_=psc_d.rearrange("l (g p) -> p l g", p=128))
        P.dma("act", ld_gprec, writes=["GPREC"], group="small")
        P.dma("act", ld_psc, writes=["PSC"], group="small")
        P.dma("act", lambda e: e.dma_start(out=ESINK[:, :, :].rearrange("p l h -> p (l h)"),
                                          in_=sink_d.rearrange("l h -> (l h)").partition_broadcast(128)),
              writes=["ESINK"], group="small")
        def load_x_group(g, gate=()):
            for i in range(4 * g, 4 * g + 4):
                P.dma("sp", lambda e, i=i: e.dma_start(out=X[:, i, :], in_=x_t[i]), reads=list(gate),
                      writes=[("X", i)], group="x%d" % i if g == 0 else "xg%d" % g)
        load_x_group(0)

        def load_consts(gate=()):
            P.dma("pool", lambda e: e.dma_start(out=BIAS[:, :, :], in_=bias_d), reads=list(gate), writes=["BIAS"], group="const")
            P.dma("pool", lambda e: e.dma_start(out=AT[:, :, :], in_=at_d), reads=list(gate), writes=["AT"], group="const")
            P.dma("pool", lambda e: e.dma_start(out=PW[:, :, :, :], in_=pw_d.rearrange("l g c d -> c l g d")),
                  reads=list(gate), writes=["PW"], group="const")

        WBLK = [(C_U, 512), (C_V, 128), (C_K, 128), (C_AG, 512), (C_PG, 512), (C_Q, 512)]

        def load_weights(l, first=False):
            wv = win_d[l].rearrange("(k p) n -> p k n", p=128)
            for bi, (c0, n) in enumerate(WBLK):
                P.dma("pool", lambda e, c0=c0, n=n: e.dma_start(out=WIN[:, :, c0:c0 + n], in_=wv[:, :, c0:c0 + n]),
                      writes=[("WIN", bi)], group="win%d_%d" % (l, bi))

        def load_wout(l, gate=()):
            wv = wout_d[l].rearrange("(k p) n -> p k n", p=128)
            P.dma("sp", lambda e: e.dma_start(out=GPOST[:, :], in_=gpost_d[l].partition_broadcast(128)),
                  reads=list(gate), writes=["GPOST"], group="gpost%d" % l)
            for hf in range(2):
                P.dma("pool", lambda e, hf=hf: e.dma_start(out=WOUT[:, :, hf * 512:(hf + 1) * 512],
                                                            in_=wv[:, :, hf * 512:(hf + 1) * 512]),
                      reads=list(gate), writes=[("WOUT", hf)], group="wout%d_%d" % (l, hf))

        WIN_KEY = {C_U: 0, C_V: 1, C_K: 2, C_AG: 3, C_PG: 4, C_Q: 5}

        def norm_chunks(l, G):
            hb = 0
            xs_c, tr_c = [], []
            for il in range(4):
                i = 4 * G + il
                xb = il

                def c_xs(i=i, xb=xb):
                    P.op("act", lambda e: e.activation(XS[:, xb, :], X[:, i, :], AF.Square,
                                                       accum_out=SSQ[:, l, i:i + 1]),
                         reads=[("X", i)], writes=[("XS", xb), ("SSQ", l, i)])
                    P.op("act", lambda e: e.activation(RSTD[:, l, i:i + 1], SSQ[:, l, i:i + 1], AF.Ln,
                                                       bias=EPSB[:, :], scale=1.0 / D),
                         reads=[("SSQ", l, i), "EPSB"], writes=[("RSTD", l, i)])
                    P.op("act", lambda e: e.activation(RSTD[:, l, i:i + 1], RSTD[:, l, i:i + 1], AF.Exp, scale=-0.5),
                         reads=[("RSTD", l, i)], writes=[("RSTD", l, i)])
                    P.op("dve", lambda e: e.tensor_scalar(XS[:, xb, :], X[:, i, :], RSTD[:, l, i:i + 1], None, ALU.mult),
                         reads=[("X", i), ("RSTD", l, i)], writes=[("XS", xb)])

                def c_tr(il=il, xb=xb):
                    tb = (0, 3)[state["tp"] % 2]
                    state["tp"] += 1

                    def tr(e):
                        tp = bank_bf16(tb).rearrange("p (k t) -> p k t", t=128)
                        for k in range(8):
                            ins = e.transpose(tp[:, k, :], XS[:, xb, k * 128:(k + 1) * 128], IDENT[:, :])
                        return ins
                    P.op("pe", tr, reads=[("XS", xb), "IDENT"], writes=[BK(tb)])
                    P.op("dve", lambda e: e.tensor_tensor(
                        HT[:, hb, :, il * 128:(il + 1) * 128],
                        bank_bf16(tb).rearrange("p (k t) -> p k t", t=128),
                        GPREC[:, l, :, None].broadcast_to([128, 8, 128]), ALU.mult),
                        reads=[BK(tb), "GPREC"], writes=[("HT", hb, il)])
                xs_c.append(c_xs); tr_c.append(c_tr)
            return xs_c, tr_c

        def gate_pipeline(b, out_ap, out_key):
            P.op("act", lambda e: e.activation(out_ap, bank_f32(b), AF.Silu),
                 reads=[BK(b)], writes=[out_key])

        def mm_tok(b, hb, il, c0, n):
            def f(e):
                for k in range(8):
                    ins = e.matmul(bank_f32(b)[:, 0:n], lhsT=HT[:, hb, k, il * 128:(il + 1) * 128],
                                   rhs=WIN[:, k, c0:c0 + n], start=(k == 0), stop=(k == 7))
                return ins
            P.op("pe", f, reads=[("HT", hb, il), ("WIN", WIN_KEY[c0])], writes=[BK(b)])

        def mm_feat(b, hb, lhs_fn, wkey):
            def f(e):
                for k in range(8):
                    ins = e.matmul(bank_f32(b), lhsT=lhs_fn(k), rhs=HT[:, hb, k, :], start=(k == 0), stop=(k == 7))
                return ins
            P.op("pe", f, reads=[("HT", hb, 0), ("HT", hb, 1), ("HT", hb, 2), ("HT", hb, 3), ("WIN", wkey)],
                 writes=[BK(b)])

        def inproj_a_chunks(l, G):
            hb = 0
            out = []
            for il in range(4):
                i = 4 * G + il
                slot = (l * NT + i) % NR

                def c_u(il=il, i=i):
                    b = alloc_gen()
                    mm_tok(b, hb, il, C_U, 512)
                    P.op("act", lambda e: e.activation(U[:, (l * NT + i) % NU, :], bank_f32(b), AF.Copy),
                         reads=[BK(b)], writes=[("U", (l * NT + i) % NU)])

                def c_v(il=il, slot=slot):
                    b = alloc_gen()
                    mm_tok(b, hb, il, C_V, 128)
                    P.op("act", lambda e: e.activation(
                        VA[:, slot, :, 0:64], bank_f32(b)[:, 0:128].rearrange("p (k d) -> p k d", d=64), AF.Copy),
                        reads=[BK(b)], writes=[("VA", slot)])
                out += [c_u, c_v]
            slot0 = (l * NT + 4 * G) % NR
            for kv in range(2):
                def c_k(kv=kv):
                    b = alloc_gen()

                    def fk(e):
                        c0 = C_K + kv * 64
                        for k in range(8):
                            e.matmul(bank_f32(b)[0:64, :], lhsT=WIN[:, k, c0:c0 + 64], rhs=HT[:, hb, k, :],
                                     start=(k == 0), stop=(k == 7))
                            ins = e.matmul(bank_f32(b)[64:128, :], lhsT=WIN[:, k, c0:c0 + 64], rhs=HT[:, hb, k, :],
                                           start=(k == 0), stop=(k == 7))
                        return ins
                    P.op("pe", fk, reads=[("HT", hb, 0), ("HT", hb, 1), ("HT", hb, 2), ("HT", hb, 3), ("WIN", WIN_KEY[C_K])],
                         writes=[BK(b)])
                    P.op("act", lambda e: e.activation(
                        KT[:, kv, slot0:slot0 + 4, :], bank_f32(b).rearrange("p (s t) -> p s t", t=128), AF.Copy),
                        reads=[BK(b)], writes=[("KT", kv, slot0 + j) for j in range(4)])
                out.append(c_k)
            return out

        def inproj_b_chunks(l, G):
            hb = 0
            qc, gc = [], []
            for c in range(4):
                def c_q(c=c):
                    b = alloc_gen()
                    mm_feat(b, hb, lambda k: WIN[:, k, C_Q + c * 128:C_Q + (c + 1) * 128], WIN_KEY[C_Q])
                    P.op("act", lambda e: e.activation(QT[:, c, :], bank_f32(b), AF.Identity, scale=0.125),
                         reads=[BK(b)], writes=[("QT", c)])
                qc.append(c_q)
            for il in range(4):
                def c_ag(il=il):
                    b = alloc_gen()
                    mm_tok(b, hb, il, C_AG, 512)
                    gate_pipeline(b, SGA[:, il, :], ("SGA", il))
                gc.append(c_ag)
            for c in range(4):
                def c_pg(c=c):
                    b = alloc_gen()
                    mm_feat(b, hb, lambda k: WIN[:, k, C_PG + c * 128:C_PG + (c + 1) * 128], WIN_KEY[C_PG])
                    gate_pipeline(b, SGP[:, c, :], ("SGP", c))
                gc.append(c_pg)
            return gc[4:] + gc[:4] + qc

        HPAIRS = [(0, 2), (1, 3), (4, 6), (5, 7)]
        HSLOT = {h: 2 * hp + j for hp, pr in enumerate(HPAIRS) for j, h in enumerate(pr)}

        def attn_chunks(l, G, splice=None):
            sc, pv, nrm, aot = {}, {}, {}, {}
            for n in range(4):
                i = 4 * G + n
                il = n
                buf = i % 2
                slot = (l * NT + i) % NR
                pslot = (l * NT + i - 1) % NR
                sc[n] = []
                for hp in range(4):
                    def c_sc(hp=hp, i=i, il=il, buf=buf, slot=slot, pslot=pslot):
                        b = 4 + state["sc"] % 2
                        state["sc"] += 1
                        h0 = HPAIRS[hp][0]

                        def f(e):
                            scv = bank_f32(b).rearrange("p (h c) -> p h c", c=256)
                            for j in range(2):
                                h = HPAIRS[hp][j]
                                kv = h // 4
                                r0 = (h % 2) * 64
                                c = h // 2
                                if i > 0:
                                    e.matmul(scv[:, j, 0:128], lhsT=KT[r0:r0 + 64, kv, pslot, :],
                                             rhs=QT[r0:r0 + 64, c, il * 128:(il + 1) * 128], start=True, stop=True)
                                ins = e.matmul(scv[:, j, 128:256], lhsT=KT[r0:r0 + 64, kv, slot, :],
                                               rhs=QT[r0:r0 + 64, c, il * 128:(il + 1) * 128], start=True, stop=True)
                            return ins
                        rd = [("QT", HPAIRS[hp][0] // 2), ("QT", HPAIRS[hp][1] // 2), ("KT", 0, slot), ("KT", 1, slot)]
                        if i > 0:
                            rd += [("KT", 0, pslot), ("KT", 1, pslot)]
                        P.op("pe", f, reads=rd, writes=[BK(b)])
                        c0 = 0 if i > 0 else 128
                        if i > 0:
                            pt_ap = PT[:, buf, 2 * hp:2 * hp + 2, :].rearrange("p h c -> p (h c)")
                            bi_ap = BIAS[:, 2 * hp:2 * hp + 2, :].rearrange("p h c -> p (h c)")
                            ps_ap = bank_f32(b)
                        else:
                            pt_ap = PT[:, buf, 2 * hp:2 * hp + 2, 128:256]
                            bi_ap = BIAS[:, 2 * hp:2 * hp + 2, 128:256]
                            ps_ap = bank_f32(b).rearrange("p (h c) -> p h c", c=256)[:, :, 128:256]
                        P.op("act", lambda e: e.activation(pt_ap, ps_ap, AF.Exp),
                             reads=[BK(b)], writes=[("PT", buf, hp)])
                        P.op("dve", lambda e: e.tensor_tensor(pt_ap, pt_ap, bi_ap, ALU.mult),
                             reads=[("PT", buf, hp), "BIAS"], writes=[("PT", buf, hp)])
                    sc[n].append(c_sc)

                pv[n] = []
                for half in range(2):
                    def c_pv(half=half, i=i, buf=buf, slot=slot, pslot=pslot):
                        b = 6 + half

                        def f(e):
                            pvv = bank_f32(b)[:, 0:260].rearrange("p (h d) -> p h d", d=65)
                            for j in range(4):
                                h = 4 * half + j
                                kv = half
                                if i > 0:
                                    e.matmul(pvv[:, j, :], lhsT=PT[:, buf, HSLOT[h], 0:128], rhs=VA[:, pslot, kv, :],
                                             start=True, stop=False)
                                ins = e.matmul(pvv[:, j, :], lhsT=PT[:, buf, HSLOT[h], 128:256], rhs=VA[:, slot, kv, :],
                                               start=(i == 0), stop=True)
                            return ins
                        rd = [("PT", buf, 2 * half), ("PT", buf, 2 * half + 1), ("VA", slot), "VA1"]
                        if i > 0:
                            rd.append(("VA", pslot))
                        P.op("pe", f, reads=rd, writes=[BK(b)])
                        P.op("dve", lambda e: e.tensor_copy(PVS[:, buf, half, :], bank_f32(b)[:, 0:260]),
                             reads=[BK(b)], writes=[("PVS", buf, half)])
                        P.op("dve", lambda e: e.tensor_tensor(
                            DEN[:, buf, 4 * half:4 * half + 4],
                            PVS[:, buf, half, :].rearrange("p (h d) -> p h d", d=65)[:, :, 64],
                            ESINK[:, l, 4 * half:4 * half + 4], ALU.add),
                            reads=[("PVS", buf, half), "ESINK"], writes=[("DEN", buf, half)])
                    pv[n].append(c_pv)

                def c_nrm(il=il, buf=buf):
                    for half in range(2):
                        P.op("dve", lambda e, half=half: e.tensor_tensor(
                            T[:, 0, half * 256:(half + 1) * 256].rearrange("p (h d) -> p h d", d=64),
                            PVS[:, buf, half, :].rearrange("p (h d) -> p h d", d=65)[:, :, 0:64],
                            SGA[:, il, half * 256:(half + 1) * 256].rearrange("p (h d) -> p h d", d=64), ALU.mult),
                            reads=[("PVS", buf, half), ("SGA", il)], writes=[("T", 0, 0)])
                    P.op("dve", lambda e: e.reciprocal(RDEN[:, buf, :], DEN[:, buf, :]),
                         reads=[("DEN", buf, 0), ("DEN", buf, 1)], writes=[("RDEN", buf)])
                    P.op("dve", lambda e: e.tensor_tensor(
                        AO[:, 0, :].rearrange("p (h d) -> p h d", d=64),
                        T[:, 0, 0:512].rearrange("p (h d) -> p h d", d=64),
                        RDEN[:, buf, :, None].broadcast_to([128, 8, 64]), ALU.mult),
                        reads=[("T", 0, 0), ("RDEN", buf)], writes=[("AO", 0, 0), ("AO", 0, 1)])
                nrm[n] = c_nrm

                def c_aot(il=il):
                    tb = (0, 3)[state["tp"] % 2]
                    state["tp"] += 1

                    def tr(e):
                        tp = bank_bf16(tb).rearrange("p (k t) -> p k t", t=128)
                        for c in range(4):
                            ins = e.transpose(tp[:, c, :], AO[:, 0, c * 128:(c + 1) * 128], IDENT[:, :])
                        return ins
                    P.op("pe", tr, reads=[("AO", 0, 0), ("AO", 0, 1), "IDENT"], writes=[BK(tb)])
                    P.op("act", lambda e: e.activation(
                        CT[:, 4:8, il * 128:(il + 1) * 128],
                        bank_bf16(tb).rearrange("p (k t) -> p k t", t=128)[:, 0:4, :], AF.Copy),
                        reads=[BK(tb)], writes=[("CTA", il)])
                aot[n] = c_aot
            if splice is None:
                seq = sc[0] + sc[1] + pv[0] + [nrm[0]] + sc[2] + pv[1] + [aot[0], nrm[1]] + sc[3] + pv[2] + \
                    [aot[1], nrm[2]] + pv[3] + [aot[2], nrm[3]]
            else:
                seq = sc[0] + sc[1] + pv[0] + [nrm[0]] + sc[2] + pv[1] + [aot[0], nrm[1]] + splice[0] + sc[3] + \
                    pv[2] + [aot[1], nrm[2]] + splice[1] + pv[3] + [aot[2], nrm[3]] + splice[2]
            return seq, aot[3]

        def pool_chunks(l, G):
            pc, wc = [], []
            for g in range(4):
                pb = g % 2

                def c_p(g=g, pb=pb):
                    b = alloc_gen()

                    def f(e):
                        for il in range(4):
                            i = 4 * G + il
                            o = bank_f32(b)[:, il * 128:(il + 1) * 128]
                            if i == 0:
                                ins = e.matmul(o, lhsT=U[:, (l * NT + i) % NU, g * 128:(g + 1) * 128],
                                               rhs=AT[:, 3 * g + 2, :], start=True, stop=True)
                            else:
                                e.matmul(o, lhsT=U[:, (l * NT + i) % NU, g * 128:(g + 1) * 128],
                                         rhs=AT[:, 3 * g + 0, :], start=True, stop=False)
                                ins = e.matmul(o, lhsT=U[:, (l * NT + i - 1) % NU, g * 128:(g + 1) * 128],
                                               rhs=AT[:, 3 * g + 1, :], start=False, stop=True)
                        return ins
                    rd = ["AT"] + [("U", (l * NT + 4 * G + il) % NU) for il in range(-1 if G > 0 else 0, 4)]
                    P.op("pe", f, reads=rd, writes=[BK(b)])
                    P.op("act", lambda e: e.activation(PLT[:, pb, :], bank_f32(b), AF.Copy),
                         reads=[BK(b)], writes=[("PLT", pb)])

                def c_w(g=g, pb=pb):
                    b2 = alloc_gen()
                    P.op("pe", lambda e: e.matmul(bank_f32(b2), lhsT=PW[:, l, g, :], rhs=PLT[:, pb, :],
                                                  start=True, stop=True),
                         reads=["PW", ("PLT", pb)], writes=[BK(b2)])
                    P.op("dve", lambda e: e.scalar_tensor_tensor(
                        CT[:, g, :], bank_f32(b2), PSC[:, l, g:g + 1], SGP[:, g, :], ALU.mult, ALU.mult),
                        reads=[BK(b2), "PSC", ("SGP", g)], writes=[("CTP", g)])
                pc.append(c_p); wc.append(c_w)
            return [pc[0], pc[1], wc[0], pc[2], wc[1], pc[3], wc[2], wc[3]]

        def out_chunks(l, G):
            out = []
            for il in range(4):
                i = 4 * G + il
                for hf in range(2):
                    def c_y(il=il, i=i, hf=hf):
                        b = alloc_gen()
                        tb = i % 2

                        def f(e):
                            for k in range(8):
                                ins = e.matmul(bank_f32(b), lhsT=CT[:, k, il * 128:(il + 1) * 128],
                                               rhs=WOUT[:, k, hf * 512:(hf + 1) * 512], start=(k == 0), stop=(k == 7))
                            return ins
                        P.op("pe", f, reads=[("CTP", 0), ("CTP", 1), ("CTP", 2), ("CTP", 3), ("CTA", il), ("WOUT", hf)],
                             writes=[BK(b)])
                        P.op("act", lambda e: e.activation(
                            PLT[:, 0, :], bank_f32(b), AF.Square, accum_out=SSQY[:, l, 2 * i + hf:2 * i + hf + 1]),
                            reads=[BK(b)], writes=[("PLT", 0), ("SSQY", l, i, hf), ("BKR", b)])
                        P.op("dve", lambda e: e.tensor_tensor(
                            T[:, tb, hf * 512:(hf + 1) * 512], bank_f32(b), GPOST[:, hf * 512:(hf + 1) * 512], ALU.mult),
                            reads=[BK(b), "GPOST"], writes=[("T", tb, hf), ("BKR", b)])
                    out.append(c_y)

                def c_fin(i=i):
                    tb = i % 2
                    P.op("act", lambda e: e.activation(RSTDY[:, l, i:i + 1], SSQY[:, l, 2 * i:2 * i + 1], AF.Identity,
                                                       bias=SSQY[:, l, 2 * i + 1:2 * i + 2], scale=1.0),
                         reads=[("SSQY", l, i, 0), ("SSQY", l, i, 1)], writes=[("RSTDY", l, i)])
                    P.op("act", lambda e: e.activation(RSTDY[:, l, i:i + 1], RSTDY[:, l, i:i + 1], AF.Ln,
                                                       bias=EPSB[:, :], scale=1.0 / D),
                         reads=[("RSTDY", l, i), "EPSB"], writes=[("RSTDY", l, i)])
                    P.op("act", lambda e: e.activation(RSTDY[:, l, i:i + 1], RSTDY[:, l, i:i + 1], AF.Exp, scale=-0.5),
                         reads=[("RSTDY", l, i)], writes=[("RSTDY", l, i)])
                    P.op("dve", lambda e: e.scalar_tensor_tensor(
                        X[:, i, :], T[:, tb, :], RSTDY[:, l, i:i + 1], X[:, i, :], ALU.mult, ALU.add),
                        reads=[("T", tb, 0), ("T", tb, 1), ("RSTDY", l, i), ("X", i)], writes=[("X", i)])
                    if l == n_layers - 1:
                        P.dma("sp", lambda e: e.dma_start(out=out_t[i], in_=X[:, i, :]), reads=[("X", i)], group="out")
                out.append(c_fin)
            return out

        def interleave(*lists):
            items_ = []
            for li, L in enumerate(lists):
                n = len(L)
                for k, c in enumerate(L):
                    items_.append(((k + 0.5) / n, li, k, c))
            items_.sort(key=lambda t: (t[0], t[1], t[2]))
            return [t[3] for t in items_]

        def run(chunks):
            for c in chunks:
                c()

        items = [(l, G) for l in range(n_layers) for G in range(n_groups)]
        load_weights(0, first=True)
        state["banks"] = [1, 2, 4, 5, 6, 7]
        xs0, tr0 = norm_chunks(0, 0)
        run([xs0[0], xs0[1], tr0[0], xs0[2], tr0[1], xs0[3], tr0[2], tr0[3]])
        load_consts(gate=[("HT", 0, 3)])
        load_wout(0, gate=[("HT", 0, 3)])
        nxg = 1
        pending_tr = None
        ipb0 = inproj_b_chunks(0, 0)
        if len(items) > 1 and n_groups > 1:
            load_x_group(1, gate=[("HT", 0, 3)])
            nxg = 2
            xs1p, pending_tr = norm_chunks(*items[1])
            run(inproj_a_chunks(0, 0) + ipb0[:8] + interleave(ipb0[8:], xs1p))
        else:
            run(inproj_a_chunks(0, 0) + ipb0)
        P.op("act", lambda e: e.activation(ESINK[:, :, :], ESINK[:, :, :], AF.Exp),
             reads=["ESINK"], writes=["ESINK"])
        for n, (l, G) in enumerate(items):
            while nxg < NG and nxg <= n + 2:
                load_x_group(nxg, gate=[("HT", 0, 3)])
                nxg += 1
            if n + 1 < len(items):
                l2, G2 = items[n + 1]
                if l2 != l:
                    load_weights(l2)
                if pending_tr is None:
                    xs1, tr1 = norm_chunks(l2, G2)
                    bstream = [xs1[0], xs1[1], xs1[2], tr1[0], xs1[3], tr1[1], tr1[2], tr1[3]]
                else:
                    bstream = pending_tr
                state["banks"] = [1, 2]
                at_seq, aot_last = attn_chunks(l, G)
                p_sc = 23
                if n > 0:
                    at_seq = at_seq[8:]
                    p_sc -= 8
                extra = []
                pending_tr = None
                if n + 2 < len(items):
                    l3, G3 = items[n + 2]
                    xs3, pending_tr = norm_chunks(l3, G3)
                    extra = xs3
                ipb = inproj_b_chunks(l2, G2)
                bfull = bstream + inproj_a_chunks(l2, G2) + extra
                nb1 = min(len(bfull), 10)
                run(interleave(at_seq[:p_sc], pool_chunks(l, G), bfull[:nb1]))
                run(interleave(at_seq[p_sc:], bfull[nb1:], ipb[8:]))
                state["banks"] = [1, 2, 6, 7]
                oc = out_chunks(l, G)
                oc_seq = oc[:5] + [aot_last] + oc[5:]
                nhead = attn_chunks(l2, G2)[0][:8]
                run(oc_seq[:6] + interleave(oc_seq[6:], nhead))
                state["banks"] = [1, 2, 4, 5, 6, 7]
                run(ipb[:8])
                if l2 != l:
                    load_wout(l2)
            else:
                state["banks"] = [1, 2]
                oc = out_chunks(l, G)
                at_seq, aot_last = attn_chunks(l, G, splice=[oc[0:3], oc[3:6], oc[6:9]])
                pcs = pool_chunks(l, G)
                if n > 0:
                    at_seq = at_seq[8:]
                    run(interleave(at_seq[:4], pcs) + at_seq[4:])
                else:
                    run(interleave(at_seq[:10], pcs) + at_seq[10:])
                state["banks"] = [1, 2, 4, 5, 6, 7]
                run([aot_last] + oc[9:])
        fw = ["out"]
        if dbg:
            dbg_list = [("U", U, [128, NU, 512]), ("QT", QT, [128, 4, 512]), ("KT", KT, [128, 2, NR, 128]),
                        ("VA", VA, [128, NR, 2, 65]), ("SGA", SGA, [128, 4, 512]), ("SGP", SGP, [128, 4, 512]),
                        ("CT", CT, [128, 8, 512]), ("HT", HT, [128, 1, 8, 512]), ("X", X, [128, NT, D]),
                        ("T", T, [128, 2, D]), ("RSTD", RSTD, [128, DEPTH, NT]), ("PT", PT, [128, 2, 8, 256]),
                        ("AO", AO, [128, 1, 512]), ("RDEN", RDEN, [128, 2, 8]), ("PLT", PLT, [128, 2, 512]),
                        ("RSTDY", RSTDY, [128, DEPTH, NT]), ("ESINK", ESINK, [128, DEPTH, 8]),
                        ("BIAS", BIAS, [128, 8, 256]), ("AT", AT, [128, 12, 128]), ("IDENT", IDENT, [128, 128])]
            allkeys = set()
            for o in P.ops:
                allkeys.update(o.writes)
            for name, tns, shape in dbg_list:
                dd = nc.dram_tensor("dbg_" + name, shape, F32, kind="ExternalOutput").ap()
                full = tuple(slice(None) for _ in shape)
                P.dma("pool", lambda e, dd=dd, tns=tns, full=full: e.dma_start(out=dd, in_=tns[full]),
                      reads=list(allkeys), group="dbg")
            fw.append("dbg")
        P.emit(final_wait_groups=fw)
    return nc


_CACHE = {}


def kernel(x, w_in, pool_w, pool_scale, attn_sinks, w_out, norm_pre, norm_post):
    if "nc" not in _CACHE:
        _CACHE["nc"] = build_program()
        _CACHE["tables"] = _const_tables()
    nc = _CACHE["nc"]
    bias, at = _CACHE["tables"]
    f = lambda a: np.ascontiguousarray(np.asarray(a, dtype=np.float32))
    shared = {"w_in": f(w_in), "pool_w": f(pool_w), "pool_scale": f(pool_scale), "attn_sinks": f(attn_sinks),
              "w_out": f(w_out), "norm_pre": f(norm_pre), "norm_post": f(norm_post), "c_bias": bias, "c_at": at}
    x = f(x)
    in_maps = [dict(shared, x=x[b]) for b in range(8)]
    res = run_bass_kernel_spmd(nc, in_maps, core_ids=list(range(8)))
    return np.stack([np.asarray(r["out"]) for r in res.results], axis=0).astype(np.float32)
```

```python
import contextlib
import numpy as np
import concourse.bass as bass
import concourse.mybir as mybir
from concourse.bass_utils import run_bass_kernel_spmd

F32 = mybir.dt.float32
BF16 = mybir.dt.bfloat16
AF = mybir.ActivationFunctionType
ALU = mybir.AluOpType

S = 2048
D = 1024
NT = 16
NG = 4
DEPTH = 2
D_IN = 2304
C_U, C_PG, C_Q, C_K, C_V, C_AG = 0, 512, 1024, 1536, 1664, 1792
MASK = -30000.0
EPS = 1e-6

COMPUTE = ("pe", "act", "dve", "pool")
QUEUES = ("pe", "act", "dve", "pool", "sp")


class Op:
    __slots__ = ("idx", "q", "fn", "reads", "writes", "is_dma", "group", "deps",
                 "eidx", "signal", "count", "name")


class Prog:
    def __init__(self, nc):
        self.nc = nc
        self.ops = []
        self.groups = {}

    def op(self, q, fn, reads=(), writes=(), name=""):
        o = Op()
        o.idx = len(self.ops); o.q = q; o.fn = fn
        o.reads = tuple(reads); o.writes = tuple(writes)
        o.is_dma = False; o.group = None; o.name = name
        self.ops.append(o)
        return o

    def dma(self, q, fn, reads=(), writes=(), group=None, name=""):
        o = self.op(q, fn, reads, writes, name)
        o.is_dma = True
        if group is None:
            group = "d%d" % o.idx
        o.group = group
        self.groups.setdefault(group, []).append(o.idx)
        return o

    def build(self):
        ops = self.ops
        last_w = {}
        readers = {}
        eng_n = {q: 0 for q in QUEUES}
        for o in ops:
            o.eidx = eng_n[o.q]; eng_n[o.q] += 1
            raw = set(); oth = set()
            for r in o.reads:
                if r in last_w:
                    raw.add(last_w[r])
            for w in o.writes:
                if w in last_w:
                    oth.add(last_w[w])
                for rd in readers.get(w, ()):
                    oth.add(rd)
            oth.discard(o.idx)
            o.deps = (raw, oth - raw)
            for r in o.reads:
                readers.setdefault(r, []).append(o.idx)
            for w in o.writes:
                last_w[w] = o.idx
                readers[w] = []
        waited = {q: {e: -1 for e in COMPUTE} for q in QUEUES}
        waited_grp = {q: set() for q in QUEUES}
        for o in ops:
            o.signal = False
        plan = []
        for o in ops:
            raw, oth = o.deps
            need_eng = {}
            need_grp = set()
            for d_idx, is_raw in [(d, True) for d in raw] + [(d, False) for d in oth]:
                d = ops[d_idx]
                if d.is_dma:
                    if d.group not in waited_grp[o.q]:
                        need_grp.add(d.group)
                    continue
                if d.q == o.q and not o.is_dma:
                    if o.q == "pe":
                        continue
                if d.eidx <= waited[o.q][d.q]:
                    continue
                if d.q not in need_eng or ops[need_eng[d.q]].eidx < d.eidx:
                    need_eng[d.q] = d_idx
            w = []
            for e, d_idx in need_eng.items():
                ops[d_idx].signal = True
                waited[o.q][e] = ops[d_idx].eidx
                w.append(("eng", e, d_idx))
            for g in sorted(need_grp):
                waited_grp[o.q].add(g)
                w.append(("grp", g))
            plan.append(w)
        cnt = {q: 0 for q in COMPUTE}
        for o in ops:
            if o.is_dma:
                continue
            if o.signal:
                cnt[o.q] += 1
                o.count = cnt[o.q]
        self.n_signals = dict(cnt)
        return plan

    def emit(self, final_wait_groups=()):
        nc = self.nc
        plan = self.build()
        ops = self.ops
        with contextlib.ExitStack() as es:
            esem = {q: es.enter_context(nc.semaphore("s_" + q)) for q in COMPUTE}
            gsem = {g: es.enter_context(nc.semaphore("g_" + g)) for g in self.groups}
            block = es.enter_context(nc.Block())

            def run(q, eng):
                for o in ops:
                    if o.q != q:
                        continue
                    for w in plan[o.idx]:
                        if w[0] == "eng":
                            eng.wait_ge(esem[w[1]], ops[w[2]].count)
                        else:
                            eng.wait_ge(gsem[w[1]], 16 * len(self.groups[w[1]]))
                    ins = o.fn(eng)
                    if o.is_dma:
                        ins.then_inc(gsem[o.group], 16)
                    elif o.signal:
                        ins.then_inc(esem[o.q], 1)
                if q == "sp":
                    for g in final_wait_groups:
                        eng.wait_ge(gsem[g], 16 * len(self.groups[g]))

            @block.sync
            def _(e):
                run("sp", e)

            @block.tensor
            def _(e):
                run("pe", e)

            @block.scalar
            def _(e):
                run("act", e)

            @block.vector
            def _(e):
                run("dve", e)

            @block.gpsimd
            def _(e):
                run("pool", e)


_HSLOT = {0: 0, 2: 1, 1: 2, 3: 3, 4: 4, 6: 5, 5: 6, 7: 7}


def _const_tables():
    slopes = 2.0 ** (-np.arange(1, 9, dtype=np.float64))
    p = np.arange(128)[:, None].astype(np.float64)
    i = np.arange(128)[None, :].astype(np.float64)
    bias = np.zeros((128, 8, 256), np.float32)
    for h in range(8):
        off = np.where(i < p, np.exp(-slopes[h] * (i + 128 - p)), 0.0)
        dg = np.where(i >= p, np.exp(-slopes[h] * (i - p)), 0.0)
        hs = _HSLOT[h]
        bias[:, hs, 0:128] = off
        bias[:, hs, 128:256] = dg
    at = np.zeros((128, 12, 128), np.float32)
    s = np.arange(128)[:, None].astype(np.float64)
    t = np.arange(128)[None, :].astype(np.float64)
    eye = (s == t).astype(np.float64)
    for g, w in enumerate((2, 4, 8, 16)):
        band = ((t - s >= 0) & (t - s < w)).astype(np.float64)
        at[:, 3 * g + 0, :] = band / w - eye
        at[:, 3 * g + 1, :] = ((t + 128 - s) < w).astype(np.float64) / w
        at[:, 3 * g + 2, :] = band / np.minimum(t + 1, w) - eye
    return bias, at


def build_program(n_layers=DEPTH, n_groups=NG, dbg=False):
    nc = bass.Bass("TRN2", target_bir_lowering=False)
    x_d = nc.dram_tensor("x", [S, D], F32, kind="ExternalInput").ap()
    win_d = nc.dram_tensor("w_in", [DEPTH, D, D_IN], F32, kind="ExternalInput").ap()
    pw_d = nc.dram_tensor("pool_w", [DEPTH, 4, 128, 128], F32, kind="ExternalInput").ap()
    psc_d = nc.dram_tensor("pool_scale", [DEPTH, 512], F32, kind="ExternalInput").ap()
    sink_d = nc.dram_tensor("attn_sinks", [DEPTH, 8], F32, kind="ExternalInput").ap()
    wout_d = nc.dram_tensor("w_out", [DEPTH, D, D], F32, kind="ExternalInput").ap()
    gpre_d = nc.dram_tensor("norm_pre", [DEPTH, D], F32, kind="ExternalInput").ap()
    gpost_d = nc.dram_tensor("norm_post", [DEPTH, D], F32, kind="ExternalInput").ap()
    bias_d = nc.dram_tensor("c_bias", [128, 8, 256], F32, kind="ExternalInput").ap()
    at_d = nc.dram_tensor("c_at", [128, 12, 128], F32, kind="ExternalInput").ap()
    out_d = nc.dram_tensor("out", [S, D], F32, kind="ExternalOutput").ap()
    x_t = x_d.rearrange("(n p) d -> n p d", p=128)
    out_t = out_d.rearrange("(n p) d -> n p d", p=128)

    with contextlib.ExitStack() as es:
        def sb(name, shape, dt):
            return es.enter_context(nc.sbuf_tensor(name, shape, dt))

        X = sb("X", [128, NT, D], F32)
        WIN = sb("WIN", [128, 8, D_IN], BF16)
        WOUT = sb("WOUT", [128, 8, D], BF16)
        PW = sb("PW", [128, DEPTH, 4, 128], BF16)
        GPREC = sb("GPREC", [128, DEPTH, 8], F32)
        GPOST = sb("GPOST", [128, D], F32)
        PSC = sb("PSC", [128, DEPTH, 4], F32)
        ESINK = sb("ESINK", [128, DEPTH, 8], F32)
        IDENT = sb("IDENT", [128, 128], BF16)
        BIAS = sb("BIAS", [128, 8, 256], BF16)
        AT = sb("AT", [128, 12, 128], BF16)
        NR = 12
        XS = sb("XS", [128, 4, D], BF16)
        HT = sb("HT", [128, 1, 8, 512], BF16)
        QT = sb("QT", [128, 4, 512], BF16)
        KT = sb("KT", [128, 2, NR, 128], BF16)
        VA = sb("VA", [128, NR, 2, 65], BF16)
        NU = 9
        U = sb("U", [128, NU, 512], BF16)
        SGP = sb("SGP", [128, 4, 512], BF16)
        SGA = sb("SGA", [128, 4, 512], BF16)
        PT = sb("PT", [128, 2, 8, 256], BF16)
        AO = sb("AO", [128, 1, 512], BF16)
        PVS = sb("PVS", [128, 2, 2, 260], F32)
        PLT = sb("PLT", [128, 2, 512], BF16)
        CT = sb("CT", [128, 8, 512], BF16)
        T = sb("T", [128, 2, D], F32)
        SSQ = sb("SSQ", [128, DEPTH, NT], F32)
        RSTD = sb("RSTD", [128, DEPTH, NT], F32)
        SSQY = sb("SSQY", [128, DEPTH, 2 * NT], F32)
        RSTDY = sb("RSTDY", [128, DEPTH, NT], F32)
        DEN = sb("DEN", [128, 2, 8], F32)
        RDEN = sb("RDEN", [128, 2, 8], F32)
        EPSB = sb("EPSB", [128, 1], F32)
        ONEB = sb("ONEB", [128, 1], F32)

        banks = [es.enter_context(nc.psum_tensor("bank%d" % b, [128, 512], F32)) for b in range(8)]

        def bank_f32(b):
            return banks[b][:, :]

        def bank_bf16(b):
            return banks[b][:, :].bitcast(BF16)

        P = Prog(nc)
        state = {"gen": 0, "gate": 0, "sc": 0, "tp": 0, "banks": [1, 2, 3]}

        def alloc_gen():
            pool_ = state["banks"]
            b = pool_[state["gen"] % len(pool_)]
            state["gen"] += 1
            return b

        def BK(b):
            return ("BK", b)

        P.op("pool", lambda e: e.memset(EPSB[:, :], EPS), writes=["EPSB"])
        P.op("pool", lambda e: e.memset(ONEB[:, :], 1.0), writes=["ONEB"])
        P.op("pool", lambda e: e.memset(SSQ[:, :, :], 0.0), writes=[("SSQ", l_, i_) for l_ in range(DEPTH) for i_ in range(NT)])
        P.op("pool", lambda e: e.memset(SSQY[:, :, :], 0.0), writes=[("SSQY", l_, i_, h_) for l_ in range(DEPTH) for i_ in range(NT) for h_ in range(2)])
        P.op("pool", lambda e: e.memset(T[:, 0, 0:128], 0.0), writes=[("T", 0, 0)])
        P.op("pool", lambda e: e.affine_select(T[:, 0, 0:128], T[:, 0, 0:128], pattern=[[-1, 128]],
                                               compare_op=ALU.not_equal, fill=1.0, base=0,
                                               channel_multiplier=1), reads=[("T", 0, 0)], writes=[("T", 0, 0)])
        P.op("pool", lambda e: e.tensor_copy(IDENT[:, :], T[:, 0, 0:128]), reads=[("T", 0, 0)], writes=["IDENT"])
        P.op("pool", lambda e: e.memset(VA[:, :, :, 64:65], 1.0), writes=["VA1"])

        def ld_gprec(e):
            with nc.allow_non_contiguous_dma(reason="tiny strided parameter loads"):
                return e.dma_start(out=GPREC[:, :, :], in_=gpre_d.rearrange("l (k p) -> p l k", p=128))
        def ld_psc(e):
            with nc.allow_non_contiguous_dma(reason="tiny strided parameter loads"):
                return e.dma_start(out=PSC[:, :, :], in_=psc_d.rearrange("l (g p) -> p l g", p=128))
        P.dma("act", ld_gprec, writes=["GPREC"], group="small")
        P.dma("act", ld_psc, writes=["PSC"], group="small")
        P.dma("act", lambda e: e.dma_start(out=ESINK[:, :, :].rearrange("p l h -> p (l h)"),
                                          in_=sink_d.rearrange("l h -> (l h)").partition_broadcast(128)),
              writes=["ESINK"], group="small")
        def load_x_group(g, gate=()):
            for i in range(4 * g, 4 * g + 4):
                P.dma("sp", lambda e, i=i: e.dma_start(out=X[:, i, :], in_=x_t[i]), reads=list(gate),
                      writes=[("X", i)], group="x%d" % i if g == 0 else "xg%d" % g)
        load_x_group(0)

        def load_consts(gate=()):
            P.dma("pool", lambda e: e.dma_start(out=BIAS[:, :, :], in_=bias_d), reads=list(gate), writes=["BIAS"], group="const")
            P.dma("pool", lambda e: e.dma_start(out=AT[:, :, :], in_=at_d), reads=list(gate), writes=["AT"], group="const")
            P.dma("pool", lambda e: e.dma_start(out=PW[:, :, :, :], in_=pw_d.rearrange("l g c d -> c l g d")),
                  reads=list(gate), writes=["PW"], group="const")

        WBLK = [(C_U, 512), (C_V, 128), (C_K, 128), (C_AG, 512), (C_PG, 512), (C_Q, 512)]

        def load_weights(l, first=False):
            wv = win_d[l].rearrange("(k p) n -> p k n", p=128)
            for bi, (c0, n) in enumerate(WBLK):
                P.dma("pool", lambda e, c0=c0, n=n: e.dma_start(out=WIN[:, :, c0:c0 + n], in_=wv[:, :, c0:c0 + n]),
                      writes=[("WIN", bi)], group="win%d_%d" % (l, bi))

        def load_wout(l, gate=()):
            wv = wout_d[l].rearrange("(k p) n -> p k n", p=128)
            P.dma("sp", lambda e: e.dma_start(out=GPOST[:, :], in_=gpost_d[l].partition_broadcast(128)),
                  reads=list(gate), writes=["GPOST"], group="gpost%d" % l)
            for hf in range(2):
                P.dma("pool", lambda e, hf=hf: e.dma_start(out=WOUT[:, :, hf * 512:(hf + 1) * 512],
                                                            in_=wv[:, :, hf * 512:(hf + 1) * 512]),
                      reads=list(gate), writes=[("WOUT", hf)], group="wout%d_%d" % (l, hf))

        WIN_KEY = {C_U: 0, C_V: 1, C_K: 2, C_AG: 3, C_PG: 4, C_Q: 5}

        def norm_chunks(l, G):
            hb = 0
            xs_c, tr_c = [], []
            for il in range(4):
                i = 4 * G + il
                xb = il

                def c_xs(i=i, xb=xb):
                    P.op("act", lambda e: e.activation(XS[:, xb, :], X[:, i, :], AF.Square,
                                                       accum_out=SSQ[:, l, i:i + 1]),
                         reads=[("X", i)], writes=[("XS", xb), ("SSQ", l, i)])
                    P.op("act", lambda e: e.activation(RSTD[:, l, i:i + 1], SSQ[:, l, i:i + 1], AF.Ln,
                                                       bias=EPSB[:, :], scale=1.0 / D),
                         reads=[("SSQ", l, i), "EPSB"], writes=[("RSTD", l, i)])
                    P.op("act", lambda e: e.activation(RSTD[:, l, i:i + 1], RSTD[:, l, i:i + 1], AF.Exp, scale=-0.5),
                         reads=[("RSTD", l, i)], writes=[("RSTD", l, i)])
                    P.op("dve", lambda e: e.tensor_scalar(XS[:, xb, :], X[:, i, :], RSTD[:, l, i:i + 1], None, ALU.mult),
                         reads=[("X", i), ("RSTD", l, i)], writes=[("XS", xb)])

                def c_tr(il=il, xb=xb):
                    tb = (0, 3)[state["tp"] % 2]
                    state["tp"] += 1

                    def tr(e):
                        tp = bank_bf16(tb).rearrange("p (k t) -> p k t", t=128)
                        for k in range(8):
                            ins = e.transpose(tp[:, k, :], XS[:, xb, k * 128:(k + 1) * 128], IDENT[:, :])
                        return ins
                    P.op("pe", tr, reads=[("XS", xb), "IDENT"], writes=[BK(tb)])
                    P.op("dve", lambda e: e.tensor_tensor(
                        HT[:, hb, :, il * 128:(il + 1) * 128],
                        bank_bf16(tb).rearrange("p (k t) -> p k t", t=128),
                        GPREC[:, l, :, None].broadcast_to([128, 8, 128]), ALU.mult),
                        reads=[BK(tb), "GPREC"], writes=[("HT", hb, il)])
                xs_c.append(c_xs); tr_c.append(c_tr)
            return xs_c, tr_c

        def gate_pipeline(b, out_ap, out_key):
            P.op("act", lambda e: e.activation(out_ap, bank_f32(b), AF.Silu),
                 reads=[BK(b)], writes=[out_key])

        def mm_tok(b, hb, il, c0, n):
            def f(e):
                for k in range(8):
                    ins = e.matmul(bank_f32(b)[:, 0:n], lhsT=HT[:, hb, k, il * 128:(il + 1) * 128],
                                   rhs=WIN[:, k, c0:c0 + n], start=(k == 0), stop=(k == 7))
                return ins
            P.op("pe", f, reads=[("HT", hb, il), ("WIN", WIN_KEY[c0])], writes=[BK(b)])

        def mm_feat(b, hb, lhs_fn, wkey):
            def f(e):
                for k in range(8):
                    ins = e.matmul(bank_f32(b), lhsT=lhs_fn(k), rhs=HT[:, hb, k, :], start=(k == 0), stop=(k == 7))
                return ins
            P.op("pe", f, reads=[("HT", hb, 0), ("HT", hb, 1), ("HT", hb, 2), ("HT", hb, 3), ("WIN", wkey)],
                 writes=[BK(b)])

        def inproj_a_chunks(l, G):
            hb = 0
            out = []
            for il in range(4):
                i = 4 * G + il
                slot = (l * NT + i) % NR

                def c_u(il=il, i=i):
                    b = alloc_gen()
                    mm_tok(b, hb, il, C_U, 512)
                    P.op("act", lambda e: e.activation(U[:, (l * NT + i) % NU, :], bank_f32(b), AF.Copy),
                         reads=[BK(b)], writes=[("U", (l * NT + i) % NU)])

                def c_v(il=il, slot=slot):
                    b = alloc_gen()
                    mm_tok(b, hb, il, C_V, 128)
                    P.op("act", lambda e: e.activation(
                        VA[:, slot, :, 0:64], bank_f32(b)[:, 0:128].rearrange("p (k d) -> p k d", d=64), AF.Copy),
                        reads=[BK(b)], writes=[("VA", slot)])
                out += [c_u, c_v]
            slot0 = (l * NT + 4 * G) % NR
            for kv in range(2):
                def c_k(kv=kv):
                    b = alloc_gen()

                    def fk(e):
                        c0 = C_K + kv * 64
                        for k in range(8):
                            e.matmul(bank_f32(b)[0:64, :], lhsT=WIN[:, k, c0:c0 + 64], rhs=HT[:, hb, k, :],
                                     start=(k == 0), stop=(k == 7))
                            ins = e.matmul(bank_f32(b)[64:128, :], lhsT=WIN[:, k, c0:c0 + 64], rhs=HT[:, hb, k, :],
                                           start=(k == 0), stop=(k == 7))
                        return ins
                    P.op("pe", fk, reads=[("HT", hb, 0), ("HT", hb, 1), ("HT", hb, 2), ("HT", hb, 3), ("WIN", WIN_KEY[C_K])],
                         writes=[BK(b)])
                    P.op("act", lambda e: e.activation(
                        KT[:, kv, slot0:slot0 + 4, :], bank_f32(b).rearrange("p (s t) -> p s t", t=128), AF.Copy),
                        reads=[BK(b)], writes=[("KT", kv, slot0 + j) for j in range(4)])
                out.append(c_k)
            return out

        def inproj_b_chunks(l, G):
            hb = 0
            qc, gc = [], []
            for c in range(4):
                def c_q(c=c):
                    b = alloc_gen()
                    mm_feat(b, hb, lambda k: WIN[:, k, C_Q + c * 128:C_Q + (c + 1) * 128], WIN_KEY[C_Q])
                    P.op("act", lambda e: e.activation(QT[:, c, :], bank_f32(b), AF.Identity, scale=0.125),
                         reads=[BK(b)], writes=[("QT", c)])
                qc.append(c_q)
            for il in range(4):
                def c_ag(il=il):
                    b = alloc_gen()
                    mm_tok(b, hb, il, C_AG, 512)
                    gate_pipeline(b, SGA[:, il, :], ("SGA", il))
                gc.append(c_ag)
            for c in range(4):
                def c_pg(c=c):
                    b = alloc_gen()
                    mm_feat(b, hb, lambda k: WIN[:, k, C_PG + c * 128:C_PG + (c + 1) * 128], WIN_KEY[C_PG])
                    gate_pipeline(b, SGP[:, c, :], ("SGP", c))
                gc.append(c_pg)
            return gc[4:] + gc[:4] + qc

        HPAIRS = [(0, 2), (1, 3), (4, 6), (5, 7)]
        HSLOT = {h: 2 * hp + j for hp, pr in enumerate(HPAIRS) for j, h in enumerate(pr)}

        def attn_chunks(l, G, splice=None):
            sc, pv, nrm, aot = {}, {}, {}, {}
            for n in range(4):
                i = 4 * G + n
                il = n
                buf = i % 2
                slot = (l * NT + i) % NR
                pslot = (l * NT + i - 1) % NR
                sc[n] = []
                for hp in range(4):
                    def c_sc(hp=hp, i=i, il=il, buf=buf, slot=slot, pslot=pslot):
                        b = 4 + state["sc"] % 2
                        state["sc"] += 1
                        h0 = HPAIRS[hp][0]

                        def f(e):
                            scv = bank_f32(b).rearrange("p (h c) -> p h c", c=256)
                            for j in range(2):
                                h = HPAIRS[hp][j]
                                kv = h // 4
                                r0 = (h % 2) * 64
                                c = h // 2
                                if i > 0:
                                    e.matmul(scv[:, j, 0:128], lhsT=KT[r0:r0 + 64, kv, pslot, :],
                                             rhs=QT[r0:r0 + 64, c, il * 128:(il + 1) * 128], start=True, stop=True)
                                ins = e.matmul(scv[:, j, 128:256], lhsT=KT[r0:r0 + 64, kv, slot, :],
                                               rhs=QT[r0:r0 + 64, c, il * 128:(il + 1) * 128], start=True, stop=True)
                            return ins
                        rd = [("QT", HPAIRS[hp][0] // 2), ("QT", HPAIRS[hp][1] // 2), ("KT", 0, slot), ("KT", 1, slot)]
                        if i > 0:
                            rd += [("KT", 0, pslot), ("KT", 1, pslot)]
                        P.op("pe", f, reads=rd, writes=[BK(b)])
                        c0 = 0 if i > 0 else 128
                        if i > 0:
                            pt_ap = PT[:, buf, 2 * hp:2 * hp + 2, :].rearrange("p h c -> p (h c)")
                            bi_ap = BIAS[:, 2 * hp:2 * hp + 2, :].rearrange("p h c -> p (h c)")
                            ps_ap = bank_f32(b)
                        else:
                            pt_ap = PT[:, buf, 2 * hp:2 * hp + 2, 128:256]
                            bi_ap = BIAS[:, 2 * hp:2 * hp + 2, 128:256]
                            ps_ap = bank_f32(b).rearrange("p (h c) -> p h c", c=256)[:, :, 128:256]
                        P.op("act", lambda e: e.activation(pt_ap, ps_ap, AF.Exp),
                             reads=[BK(b)], writes=[("PT", buf, hp)])
                        P.op("dve", lambda e: e.tensor_tensor(pt_ap, pt_ap, bi_ap, ALU.mult),
                             reads=[("PT", buf, hp), "BIAS"], writes=[("PT", buf, hp)])
                    sc[n].append(c_sc)

                pv[n] = []
                for half in range(2):
                    def c_pv(half=half, i=i, buf=buf, slot=slot, pslot=pslot):
                        b = 6 + half

                        def f(e):
                            pvv = bank_f32(b)[:, 0:260].rearrange("p (h d) -> p h d", d=65)
                            for j in range(4):
                                h = 4 * half + j
                                kv = half
                                if i > 0:
                                    e.matmul(pvv[:, j, :], lhsT=PT[:, buf, HSLOT[h], 0:128], rhs=VA[:, pslot, kv, :],
                                             start=True, stop=False)
                                ins = e.matmul(pvv[:, j, :], lhsT=PT[:, buf, HSLOT[h], 128:256], rhs=VA[:, slot, kv, :],
                                               start=(i == 0), stop=True)
                            return ins
                        rd = [("PT", buf, 2 * half), ("PT", buf, 2 * half + 1), ("VA", slot), "VA1"]
                        if i > 0:
                            rd.append(("VA", pslot))
                        P.op("pe", f, reads=rd, writes=[BK(b)])
                        P.op("dve", lambda e: e.tensor_copy(PVS[:, buf, half, :], bank_f32(b)[:, 0:260]),
                             reads=[BK(b)], writes=[("PVS", buf, half)])
                        P.op("dve", lambda e: e.tensor_tensor(
                            DEN[:, buf, 4 * half:4 * half + 4],
                            PVS[:, buf, half, :].rearrange("p (h d) -> p h d", d=65)[:, :, 64],
                            ESINK[:, l, 4 * half:4 * half + 4], ALU.add),
                            reads=[("PVS", buf, half), "ESINK"], writes=[("DEN", buf, half)])
                    pv[n].append(c_pv)

                def c_nrm(il=il, buf=buf):
                    for half in range(2):
                        P.op("dve", lambda e, half=half: e.tensor_tensor(
                            T[:, 0, half * 256:(half + 1) * 256].rearrange("p (h d) -> p h d", d=64),
                            PVS[:, buf, half, :].rearrange("p (h d) -> p h d", d=65)[:, :, 0:64],
                            SGA[:, il, half * 256:(half + 1) * 256].rearrange("p (h d) -> p h d", d=64), ALU.mult),
                            reads=[("PVS", buf, half), ("SGA", il)], writes=[("T", 0, 0)])
                    P.op("dve", lambda e: e.reciprocal(RDEN[:, buf, :], DEN[:, buf, :]),
                         reads=[("DEN", buf, 0), ("DEN", buf, 1)], writes=[("RDEN", buf)])
                    P.op("dve", lambda e: e.tensor_tensor(
                        AO[:, 0, :].rearrange("p (h d) -> p h d", d=64),
                        T[:, 0, 0:512].rearrange("p (h d) -> p h d", d=64),
                        RDEN[:, buf, :, None].broadcast_to([128, 8, 64]), ALU.mult),
                        reads=[("T", 0, 0), ("RDEN", buf)], writes=[("AO", 0, 0), ("AO", 0, 1)])
                nrm[n] = c_nrm

                def c_aot(il=il):
                    tb = (0, 3)[state["tp"] % 2]
                    state["tp"] += 1

                    def tr(e):
                        tp = bank_bf16(tb).rearrange("p (k t) -> p k t", t=128)
                        for c in range(4):
                            ins = e.transpose(tp[:, c, :], AO[:, 0, c * 128:(c + 1) * 128], IDENT[:, :])
                        return ins
                    P.op("pe", tr, reads=[("AO", 0, 0), ("AO", 0, 1), "IDENT"], writes=[BK(tb)])
                    P.op("act", lambda e: e.activation(
                        CT[:, 4:8, il * 128:(il + 1) * 128],
                        bank_bf16(tb).rearrange("p (k t) -> p k t", t=128)[:, 0:4, :], AF.Copy),
                        reads=[BK(tb)], writes=[("CTA", il)])
                aot[n] = c_aot
            if splice is None:
                seq = sc[0] + sc[1] + pv[0] + [nrm[0]] + sc[2] + pv[1] + [aot[0], nrm[1]] + sc[3] + pv[2] + \
                    [aot[1], nrm[2]] + pv[3] + [aot[2], nrm[3]]
            else:
                seq = sc[0] + sc[1] + pv[0] + [nrm[0]] + sc[2] + pv[1] + [aot[0], nrm[1]] + splice[0] + sc[3] + \
                    pv[2] + [aot[1], nrm[2]] + splice[1] + pv[3] + [aot[2], nrm[3]] + splice[2]
            return seq, aot[3]

        def pool_chunks(l, G):
            pc, wc = [], []
            for g in range(4):
                pb = g % 2

                def c_p(g=g, pb=pb):
                    b = alloc_gen()

                    def f(e):
                        for il in range(4):
                            i = 4 * G + il
                            o = bank_f32(b)[:, il * 128:(il + 1) * 128]
                            if i == 0:
                                ins = e.matmul(o, lhsT=U[:, (l * NT + i) % NU, g * 128:(g + 1) * 128],
                                               rhs=AT[:, 3 * g + 2, :], start=True, stop=True)
                            else:
                                e.matmul(o, lhsT=U[:, (l * NT + i) % NU, g * 128:(g + 1) * 128],
                                         rhs=AT[:, 3 * g + 0, :], start=True, stop=False)
                                ins = e.matmul(o, lhsT=U[:, (l * NT + i - 1) % NU, g * 128:(g + 1) * 128],
                                               rhs=AT[:, 3 * g + 1, :], start=False, stop=True)
                        return ins
                    rd = ["AT"] + [("U", (l * NT + 4 * G + il) % NU) for il in range(-1 if G > 0 else 0, 4)]
                    P.op("pe", f, reads=rd, writes=[BK(b)])
                    P.op("act", lambda e: e.activation(PLT[:, pb, :], bank_f32(b), AF.Copy),
                         reads=[BK(b)], writes=[("PLT", pb)])

                def c_w(g=g, pb=pb):
                    b2 = alloc_gen()
                    P.op("pe", lambda e: e.matmul(bank_f32(b2), lhsT=PW[:, l, g, :], rhs=PLT[:, pb, :],
                                                  start=True, stop=True),
                         reads=["PW", ("PLT", pb)], writes=[BK(b2)])
                    P.op("dve", lambda e: e.scalar_tensor_tensor(
                        CT[:, g, :], bank_f32(b2), PSC[:, l, g:g + 1], SGP[:, g, :], ALU.mult, ALU.mult),
                        reads=[BK(b2), "PSC", ("SGP", g)], writes=[("CTP", g)])
                pc.append(c_p); wc.append(c_w)
            return [pc[0], pc[1], wc[0], pc[2], wc[1], pc[3], wc[2], wc[3]]

        def out_chunks(l, G):
            out = []
            for il in range(4):
                i = 4 * G + il
                for hf in range(2):
                    def c_y(il=il, i=i, hf=hf):
                        b = alloc_gen()
                        tb = i % 2

                        def f(e):
                            for k in range(8):
                                ins = e.matmul(bank_f32(b), lhsT=CT[:, k, il * 128:(il + 1) * 128],
                                               rhs=WOUT[:, k, hf * 512:(hf + 1) * 512], start=(k == 0), stop=(k == 7))
                            return ins
                        P.op("pe", f, reads=[("CTP", 0), ("CTP", 1), ("CTP", 2), ("CTP", 3), ("CTA", il), ("WOUT", hf)],
                             writes=[BK(b)])
                        P.op("act", lambda e: e.activation(
                            PLT[:, 0, :], bank_f32(b), AF.Square, accum_out=SSQY[:, l, 2 * i + hf:2 * i + hf + 1]),
                            reads=[BK(b)], writes=[("PLT", 0), ("SSQY", l, i, hf), ("BKR", b)])
                        P.op("dve", lambda e: e.tensor_tensor(
                            T[:, tb, hf * 512:(hf + 1) * 512], bank_f32(b), GPOST[:, hf * 512:(hf + 1) * 512], ALU.mult),
                            reads=[BK(b), "GPOST"], writes=[("T", tb, hf), ("BKR", b)])
                    out.append(c_y)

                def c_fin(i=i):
                    tb = i % 2
                    P.op("act", lambda e: e.activation(RSTDY[:, l, i:i + 1], SSQY[:, l, 2 * i:2 * i + 1], AF.Identity,
                                                       bias=SSQY[:, l, 2 * i + 1:2 * i + 2], scale=1.0),
                         reads=[("SSQY", l, i, 0), ("SSQY", l, i, 1)], writes=[("RSTDY", l, i)])
                    P.op("act", lambda e: e.activation(RSTDY[:, l, i:i + 1], RSTDY[:, l, i:i + 1], AF.Ln,
                                                       bias=EPSB[:, :], scale=1.0 / D),
                         reads=[("RSTDY", l, i), "EPSB"], writes=[("RSTDY", l, i)])
                    P.op("act", lambda e: e.activation(RSTDY[:, l, i:i + 1], RSTDY[:, l, i:i + 1], AF.Exp, scale=-0.5),
                         reads=[("RSTDY", l, i)], writes=[("RSTDY", l, i)])
                    P.op("dve", lambda e: e.scalar_tensor_tensor(
                        X[:, i, :], T[:, tb, :], RSTDY[:, l, i:i + 1], X[:, i, :], ALU.mult, ALU.add),
                        reads=[("T", tb, 0), ("T", tb, 1), ("RSTDY", l, i), ("X", i)], writes=[("X", i)])
                    if l == n_layers - 1:
                        P.dma("sp", lambda e: e.dma_start(out=out_t[i], in_=X[:, i, :]), reads=[("X", i)], group="out")
                out.append(c_fin)
            return out

        def interleave(*lists):
            items_ = []
            for li, L in enumerate(lists):
                n = len(L)
                for k, c in enumerate(L):
                    items_.append(((k + 0.5) / n, li, k, c))
            items_.sort(key=lambda t: (t[0], t[1], t[2]))
            return [t[3] for t in items_]

        def run(chunks):
            for c in chunks:
                c()

        items = [(l, G) for l in range(n_layers) for G in range(n_groups)]
        load_weights(0, first=True)
        state["banks"] = [1, 2, 4, 5, 6, 7]
        xs0, tr0 = norm_chunks(0, 0)
        run([xs0[0], xs0[1], tr0[0], xs0[2], tr0[1], xs0[3], tr0[2], tr0[3]])
        load_consts(gate=[("HT", 0, 3)])
        load_wout(0, gate=[("HT", 0, 3)])
        nxg = 1
        pending_tr = None
        ipb0 = inproj_b_chunks(0, 0)
        if len(items) > 1 and n_groups > 1:
            load_x_group(1, gate=[("HT", 0, 3)])
            nxg = 2
            xs1p, pending_tr = norm_chunks(*items[1])
            run(inproj_a_chunks(0, 0) + ipb0[:8] + interleave(ipb0[8:], xs1p))
        else:
            run(inproj_a_chunks(0, 0) + ipb0)
        P.op("act", lambda e: e.activation(ESINK[:, :, :], ESINK[:, :, :], AF.Exp),
             reads=["ESINK"], writes=["ESINK"])
        for n, (l, G) in enumerate(items):
            while nxg < NG and nxg <= n + 2:
                load_x_group(nxg, gate=[("HT", 0, 3)])
                nxg += 1
            if n + 1 < len(items):
                l2, G2 = items[n + 1]
                if l2 != l:
                    load_weights(l2)
                if pending_tr is None:
                    xs1, tr1 = norm_chunks(l2, G2)
                    bstream = [xs1[0], xs1[1], xs1[2], tr1[0], xs1[3], tr1[1], tr1[2], tr1[3]]
                else:
                    bstream = pending_tr
                state["banks"] = [1, 2]
                at_seq, aot_last = attn_chunks(l, G)
                p_sc = 23
                if n > 0:
                    at_seq = at_seq[10:]
                    p_sc -= 10
                extra = []
                pending_tr = None
                if n + 2 < len(items):
                    l3, G3 = items[n + 2]
                    xs3, pending_tr = norm_chunks(l3, G3)
                    extra = xs3
                ipb = inproj_b_chunks(l2, G2)
                bfull = bstream + inproj_a_chunks(l2, G2) + extra
                nb1 = (len(bfull) * p_sc) // len(at_seq)
                run(interleave(at_seq[:p_sc], pool_chunks(l, G), bfull[:nb1]))
                run(interleave(at_seq[p_sc:], bfull[nb1:], ipb[8:]))
                state["banks"] = [1, 2, 6, 7]
                oc = out_chunks(l, G)
                oc_seq = oc[:5] + [aot_last] + oc[5:]
                nseq = attn_chunks(l2, G2)[0]
                nhead = nseq[:8]
                run(oc_seq[:6] + interleave(oc_seq[6:], nhead))
                state["banks"] = [1, 2, 4, 5]
                run(ipb[:8] + nseq[8:10])
                if l2 != l:
                    load_wout(l2)
            else:
                state["banks"] = [1, 2]
                oc = out_chunks(l, G)
                at_seq, aot_last = attn_chunks(l, G, splice=[oc[0:3], oc[3:6], oc[6:9]])
                pcs = pool_chunks(l, G)
                if n > 0:
                    at_seq = at_seq[10:]
                    run(interleave(at_seq[:2], pcs) + at_seq[2:])
                else:
                    run(interleave(at_seq[:10], pcs) + at_seq[10:])
                state["banks"] = [1, 2, 4, 5, 6, 7]
                run([aot_last] + oc[9:])
        fw = ["out"]
        if dbg:
            dbg_list = [("U", U, [128, NU, 512]), ("QT", QT, [128, 4, 512]), ("KT", KT, [128, 2, NR, 128]),
                        ("VA", VA, [128, NR, 2, 65]), ("SGA", SGA, [128, 4, 512]), ("SGP", SGP, [128, 4, 512]),
                        ("CT", CT, [128, 8, 512]), ("HT", HT, [128, 1, 8, 512]), ("X", X, [128, NT, D]),
                        ("T", T, [128, 2, D]), ("RSTD", RSTD, [128, DEPTH, NT]), ("PT", PT, [128, 2, 8, 256]),
                        ("AO", AO, [128, 1, 512]), ("RDEN", RDEN, [128, 2, 8]), ("PLT", PLT, [128, 2, 512]),
                        ("RSTDY", RSTDY, [128, DEPTH, NT]), ("ESINK", ESINK, [128, DEPTH, 8]),
                        ("BIAS", BIAS, [128, 8, 256]), ("AT", AT, [128, 12, 128]), ("IDENT", IDENT, [128, 128])]
            allkeys = set()
            for o in P.ops:
                allkeys.update(o.writes)
            for name, tns, shape in dbg_list:
                dd = nc.dram_tensor("dbg_" + name, shape, F32, kind="ExternalOutput").ap()
                full = tuple(slice(None) for _ in shape)
                P.dma("pool", lambda e, dd=dd, tns=tns, full=full: e.dma_start(out=dd, in_=tns[full]),
                      reads=list(allkeys), group="dbg")
            fw.append("dbg")
        P.emit(final_wait_groups=fw)
    return nc


_CACHE = {}


def kernel(x, w_in, pool_w, pool_scale, attn_sinks, w_out, norm_pre, norm_post):
    if "nc" not in _CACHE:
        _CACHE["nc"] = build_program()
        _CACHE["tables"] = _const_tables()
    nc = _CACHE["nc"]
    bias, at = _CACHE["tables"]
    f = lambda a: np.ascontiguousarray(np.asarray(a, dtype=np.float32))
    shared = {"w_in": f(w_in), "pool_w": f(pool_w), "pool_scale": f(pool_scale), "attn_sinks": f(attn_sinks),
              "w_out": f(w_out), "norm_pre": f(norm_pre), "norm_post": f(norm_post), "c_bias": bias, "c_at": at}
    x = f(x)
    in_maps = [dict(shared, x=x[b]) for b in range(8)]
    res = run_bass_kernel_spmd(nc, in_maps, core_ids=list(range(8)))
    return np.stack([np.asarray(r["out"]) for r in res.results], axis=0).astype(np.float32)
```

```python
import contextlib
import numpy as np
import concourse.bass as bass
import concourse.mybir as mybir
from concourse.bass_utils import run_bass_kernel_spmd

F32 = mybir.dt.float32
BF16 = mybir.dt.bfloat16
AF = mybir.ActivationFunctionType
ALU = mybir.AluOpType

S = 2048
D = 1024
NT = 16
NG = 4
DEPTH = 2
D_IN = 2304
C_U, C_PG, C_Q, C_K, C_V, C_AG = 0, 512, 1024, 1536, 1664, 1792
MASK = -30000.0
EPS = 1e-6

COMPUTE = ("pe", "act", "dve", "pool")
QUEUES = ("pe", "act", "dve", "pool", "sp")


class Op:
    __slots__ = ("idx", "q", "fn", "reads", "writes", "is_dma", "group", "deps",
                 "eidx", "signal", "count", "name")


class Prog:
    def __init__(self, nc):
        self.nc = nc
        self.ops = []
        self.groups = {}

    def op(self, q, fn, reads=(), writes=(), name=""):
        o = Op()
        o.idx = len(self.ops); o.q = q; o.fn = fn
        o.reads = tuple(reads); o.writes = tuple(writes)
        o.is_dma = False; o.group = None; o.name = name
        self.ops.append(o)
        return o

    def dma(self, q, fn, reads=(), writes=(), group=None, name=""):
        o = self.op(q, fn, reads, writes, name)
        o.is_dma = True
        if group is None:
            group = "d%d" % o.idx
        o.group = group
        self.groups.setdefault(group, []).append(o.idx)
        return o

    def build(self):
        ops = self.ops
        last_w = {}
        readers = {}
        eng_n = {q: 0 for q in QUEUES}
        for o in ops:
            o.eidx = eng_n[o.q]; eng_n[o.q] += 1
            raw = set(); oth = set()
            for r in o.reads:
                if r in last_w:
                    raw.add(last_w[r])
            for w in o.writes:
                if w in last_w:
                    oth.add(last_w[w])
                for rd in readers.get(w, ()):
                    oth.add(rd)
            oth.discard(o.idx)
            o.deps = (raw, oth - raw)
            for r in o.reads:
                readers.setdefault(r, []).append(o.idx)
            for w in o.writes:
                last_w[w] = o.idx
                readers[w] = []
        waited = {q: {e: -1 for e in COMPUTE} for q in QUEUES}
        waited_grp = {q: set() for q in QUEUES}
        for o in ops:
            o.signal = False
        plan = []
        for o in ops:
            raw, oth = o.deps
            need_eng = {}
            need_grp = set()
            for d_idx, is_raw in [(d, True) for d in raw] + [(d, False) for d in oth]:
                d = ops[d_idx]
                if d.is_dma:
                    if d.group not in waited_grp[o.q]:
                        need_grp.add(d.group)
                    continue
                if d.q == o.q and not o.is_dma:
                    if o.q == "pe":
                        continue
                if d.eidx <= waited[o.q][d.q]:
                    continue
                if d.q not in need_eng or ops[need_eng[d.q]].eidx < d.eidx:
                    need_eng[d.q] = d_idx
            w = []
            for e, d_idx in need_eng.items():
                ops[d_idx].signal = True
                waited[o.q][e] = ops[d_idx].eidx
                w.append(("eng", e, d_idx))
            for g in sorted(need_grp):
                waited_grp[o.q].add(g)
                w.append(("grp", g))
            plan.append(w)
        cnt = {q: 0 for q in COMPUTE}
        for o in ops:
            if o.is_dma:
                continue
            if o.signal:
                cnt[o.q] += 1
                o.count = cnt[o.q]
        self.n_signals = dict(cnt)
        return plan

    def emit(self, final_wait_groups=()):
        nc = self.nc
        plan = self.build()
        ops = self.ops
        with contextlib.ExitStack() as es:
            esem = {q: es.enter_context(nc.semaphore("s_" + q)) for q in COMPUTE}
            gsem = {g: es.enter_context(nc.semaphore("g_" + g)) for g in self.groups}
            block = es.enter_context(nc.Block())

            def run(q, eng):
                for o in ops:
                    if o.q != q:
                        continue
                    for w in plan[o.idx]:
                        if w[0] == "eng":
                            eng.wait_ge(esem[w[1]], ops[w[2]].count)
                        else:
                            eng.wait_ge(gsem[w[1]], 16 * len(self.groups[w[1]]))
                    ins = o.fn(eng)
                    if o.is_dma:
                        ins.then_inc(gsem[o.group], 16)
                    elif o.signal:
                        ins.then_inc(esem[o.q], 1)
                if q == "sp":
                    for g in final_wait_groups:
                        eng.wait_ge(gsem[g], 16 * len(self.groups[g]))

            @block.sync
            def _(e):
                run("sp", e)

            @block.tensor
            def _(e):
                run("pe", e)

            @block.scalar
            def _(e):
                run("act", e)

            @block.vector
            def _(e):
                run("dve", e)

            @block.gpsimd
            def _(e):
                run("pool", e)


_HSLOT = {0: 0, 2: 1, 1: 2, 3: 3, 4: 4, 6: 5, 5: 6, 7: 7}


def _const_tables():
    slopes = 2.0 ** (-np.arange(1, 9, dtype=np.float64))
    p = np.arange(128)[:, None].astype(np.float64)
    i = np.arange(128)[None, :].astype(np.float64)
    bias = np.zeros((128, 8, 256), np.float32)
    for h in range(8):
        off = np.where(i < p, np.exp(-slopes[h] * (i + 128 - p)), 0.0)
        dg = np.where(i >= p, np.exp(-slopes[h] * (i - p)), 0.0)
        hs = _HSLOT[h]
        bias[:, hs, 0:128] = off
        bias[:, hs, 128:256] = dg
    at = np.zeros((128, 12, 128), np.float32)
    s = np.arange(128)[:, None].astype(np.float64)
    t = np.arange(128)[None, :].astype(np.float64)
    eye = (s == t).astype(np.float64)
    for g, w in enumerate((2, 4, 8, 16)):
        band = ((t - s >= 0) & (t - s < w)).astype(np.float64)
        at[:, 3 * g + 0, :] = band / w - eye
        at[:, 3 * g + 1, :] = ((t + 128 - s) < w).astype(np.float64) / w
        at[:, 3 * g + 2, :] = band / np.minimum(t + 1, w) - eye
    return bias, at


def build_program(n_layers=DEPTH, n_groups=NG, dbg=False):
    nc = bass.Bass("TRN2", target_bir_lowering=False)
    x_d = nc.dram_tensor("x", [S, D], F32, kind="ExternalInput").ap()
    win_d = nc.dram_tensor("w_in", [DEPTH, D, D_IN], F32, kind="ExternalInput").ap()
    pw_d = nc.dram_tensor("pool_w", [DEPTH, 4, 128, 128], F32, kind="ExternalInput").ap()
    psc_d = nc.dram_tensor("pool_scale", [DEPTH, 512], F32, kind="ExternalInput").ap()
    sink_d = nc.dram_tensor("attn_sinks", [DEPTH, 8], F32, kind="ExternalInput").ap()
    wout_d = nc.dram_tensor("w_out", [DEPTH, D, D], F32, kind="ExternalInput").ap()
    gpre_d = nc.dram_tensor("norm_pre", [DEPTH, D], F32, kind="ExternalInput").ap()
    gpost_d = nc.dram_tensor("norm_post", [DEPTH, D], F32, kind="ExternalInput").ap()
    bias_d = nc.dram_tensor("c_bias", [128, 8, 256], F32, kind="ExternalInput").ap()
    at_d = nc.dram_tensor("c_at", [128, 12, 128], F32, kind="ExternalInput").ap()
    out_d = nc.dram_tensor("out", [S, D], F32, kind="ExternalOutput").ap()
    x_t = x_d.rearrange("(n p) d -> n p d", p=128)
    out_t = out_d.rearrange("(n p) d -> n p d", p=128)

    with contextlib.ExitStack() as es:
        def sb(name, shape, dt):
            return es.enter_context(nc.sbuf_tensor(name, shape, dt))

        X = sb("X", [128, NT, D], F32)
        WIN = sb("WIN", [128, 8, D_IN], BF16)
        WOUT = sb("WOUT", [128, 8, D], BF16)
        PW = sb("PW", [128, DEPTH, 4, 128], BF16)
        GPREC = sb("GPREC", [128, DEPTH, 8], F32)
        GPOST = sb("GPOST", [128, D], F32)
        PSC = sb("PSC", [128, DEPTH, 4], F32)
        ESINK = sb("ESINK", [128, DEPTH, 8], F32)
        IDENT = sb("IDENT", [128, 128], BF16)
        BIAS = sb("BIAS", [128, 8, 256], BF16)
        AT = sb("AT", [128, 12, 128], BF16)
        NR = 12
        XS = sb("XS", [128, 4, D], BF16)
        HT = sb("HT", [128, 1, 8, 512], BF16)
        QT = sb("QT", [128, 4, 512], BF16)
        KT = sb("KT", [128, 2, NR, 128], BF16)
        VA = sb("VA", [128, NR, 2, 65], BF16)
        NU = 9
        U = sb("U", [128, NU, 512], BF16)
        SGP = sb("SGP", [128, 4, 512], BF16)
        SGA = sb("SGA", [128, 4, 512], BF16)
        PT = sb("PT", [128, 2, 8, 256], BF16)
        AO = sb("AO", [128, 1, 512], BF16)
        PVS = sb("PVS", [128, 2, 2, 260], F32)
        PLT = sb("PLT", [128, 2, 512], BF16)
        CT = sb("CT", [128, 8, 512], BF16)
        T = sb("T", [128, 2, D], F32)
        SSQ = sb("SSQ", [128, DEPTH, NT], F32)
        RSTD = sb("RSTD", [128, DEPTH, NT], F32)
        SSQY = sb("SSQY", [128, DEPTH, 2 * NT], F32)
        RSTDY = sb("RSTDY", [128, DEPTH, NT], F32)
        DEN = sb("DEN", [128, 2, 8], F32)
        RDEN = sb("RDEN", [128, 2, 8], F32)
        EPSB = sb("EPSB", [128, 1], F32)
        ONEB = sb("ONEB", [128, 1], F32)

        banks = [es.enter_context(nc.psum_tensor("bank%d" % b, [128, 512], F32)) for b in range(8)]

        def bank_f32(b):
            return banks[b][:, :]

        def bank_bf16(b):
            return banks[b][:, :].bitcast(BF16)

        P = Prog(nc)
        state = {"gen": 0, "gate": 0, "sc": 0, "tp": 0, "banks": [1, 2, 3]}

        def alloc_gen():
            pool_ = state["banks"]
            b = pool_[state["gen"] % len(pool_)]
            state["gen"] += 1
            return b

        def BK(b):
            return ("BK", b)

        P.op("pool", lambda e: e.memset(EPSB[:, :], EPS), writes=["EPSB"])
        P.op("pool", lambda e: e.memset(ONEB[:, :], 1.0), writes=["ONEB"])
        P.op("act", lambda e: e.activation(DEN[:, 0, 0:1], ONEB[:, :], AF.Exp), reads=["ONEB"],
             writes=[("DEN", 0, 0)])
        P.op("pool", lambda e: e.memset(SSQ[:, :, :], 0.0), writes=[("SSQ", l_, i_) for l_ in range(DEPTH) for i_ in range(NT)])
        P.op("pool", lambda e: e.memset(SSQY[:, :, :], 0.0), writes=[("SSQY", l_, i_, h_) for l_ in range(DEPTH) for i_ in range(NT) for h_ in range(2)])
        P.op("pool", lambda e: e.memset(T[:, 0, 0:128], 0.0), writes=[("T", 0, 0)])
        P.op("pool", lambda e: e.affine_select(T[:, 0, 0:128], T[:, 0, 0:128], pattern=[[-1, 128]],
                                               compare_op=ALU.not_equal, fill=1.0, base=0,
                                               channel_multiplier=1), reads=[("T", 0, 0)], writes=[("T", 0, 0)])
        P.op("pool", lambda e: e.tensor_copy(IDENT[:, :], T[:, 0, 0:128]), reads=[("T", 0, 0)], writes=["IDENT"])
        P.op("pool", lambda e: e.memset(VA[:, :, :, 64:65], 1.0), writes=["VA1"])

        def ld_gprec(e):
            with nc.allow_non_contiguous_dma(reason="tiny strided parameter loads"):
                return e.dma_start(out=GPREC[:, :, :], in_=gpre_d.rearrange("l (k p) -> p l k", p=128))
        def ld_psc(e):
            with nc.allow_non_contiguous_dma(reason="tiny strided parameter loads"):
                return e.dma_start(out=PSC[:, :, :], in_=psc_d.rearrange("l (g p) -> p l g", p=128))
        P.dma("act", ld_gprec, writes=["GPREC"], group="small")
        P.dma("act", ld_psc, writes=["PSC"], group="small")
        P.dma("act", lambda e: e.dma_start(out=ESINK[:, :, :].rearrange("p l h -> p (l h)"),
                                          in_=sink_d.rearrange("l h -> (l h)").partition_broadcast(128)),
              writes=["ESINK"], group="small")
        def load_x_group(g, gate=()):
            for i in range(4 * g, 4 * g + 4):
                P.dma("sp", lambda e, i=i: e.dma_start(out=X[:, i, :], in_=x_t[i]), reads=list(gate),
                      writes=[("X", i)], group="x%d" % i if g == 0 else "xg%d" % g)
        load_x_group(0)

        def load_consts(gate=()):
            P.dma("pool", lambda e: e.dma_start(out=BIAS[:, :, :], in_=bias_d), reads=list(gate), writes=["BIAS"], group="const")
            P.dma("pool", lambda e: e.dma_start(out=AT[:, :, :], in_=at_d), reads=list(gate), writes=["AT"], group="const")
            P.dma("pool", lambda e: e.dma_start(out=PW[:, :, :, :], in_=pw_d.rearrange("l g c d -> c l g d")),
                  reads=list(gate), writes=["PW"], group="const")

        WBLK = [(C_U, 512), (C_V, 128), (C_K, 128), (C_AG, 512), (C_PG, 512), (C_Q, 512)]

        def load_weights(l, first=False):
            wv = win_d[l].rearrange("(k p) n -> p k n", p=128)
            for bi, (c0, n) in enumerate(WBLK):
                P.dma("pool", lambda e, c0=c0, n=n: e.dma_start(out=WIN[:, :, c0:c0 + n], in_=wv[:, :, c0:c0 + n]),
                      writes=[("WIN", bi)], group="win%d_%d" % (l, bi))

        def load_wout(l, gate=()):
            wv = wout_d[l].rearrange("(k p) n -> p k n", p=128)
            P.dma("sp", lambda e: e.dma_start(out=GPOST[:, :], in_=gpost_d[l].partition_broadcast(128)),
                  reads=list(gate), writes=["GPOST"], group="gpost%d" % l)
            for hf in range(2):
                P.dma("pool", lambda e, hf=hf: e.dma_start(out=WOUT[:, :, hf * 512:(hf + 1) * 512],
                                                            in_=wv[:, :, hf * 512:(hf + 1) * 512]),
                      reads=list(gate), writes=[("WOUT", hf)], group="wout%d_%d" % (l, hf))

        WIN_KEY = {C_U: 0, C_V: 1, C_K: 2, C_AG: 3, C_PG: 4, C_Q: 5}

        def norm_chunks(l, G):
            hb = 0
            xs_c, tr_c = [], []
            for il in range(4):
                i = 4 * G + il
                xb = il

                def c_xs(i=i, xb=xb):
                    P.op("act", lambda e: e.activation(XS[:, xb, :], X[:, i, :], AF.Square,
                                                       accum_out=SSQ[:, l, i:i + 1]),
                         reads=[("X", i)], writes=[("XS", xb), ("SSQ", l, i)])
                    P.op("act", lambda e: e.activation(RSTD[:, l, i:i + 1], SSQ[:, l, i:i + 1], AF.Ln,
                                                       bias=EPSB[:, :], scale=1.0 / D),
                         reads=[("SSQ", l, i), "EPSB"], writes=[("RSTD", l, i)])
                    P.op("act", lambda e: e.activation(RSTD[:, l, i:i + 1], RSTD[:, l, i:i + 1], AF.Exp, scale=-0.5),
                         reads=[("RSTD", l, i)], writes=[("RSTD", l, i)])
                    P.op("dve", lambda e: e.tensor_scalar(XS[:, xb, :], X[:, i, :], RSTD[:, l, i:i + 1], None, ALU.mult),
                         reads=[("X", i), ("RSTD", l, i)], writes=[("XS", xb)])

                def c_tr(il=il, xb=xb):
                    tb = (0, 3)[state["tp"] % 2]
                    state["tp"] += 1

                    def tr(e):
                        tp = bank_bf16(tb).rearrange("p (k t) -> p k t", t=128)
                        for k in range(8):
                            ins = e.transpose(tp[:, k, :], XS[:, xb, k * 128:(k + 1) * 128], IDENT[:, :])
                        return ins
                    P.op("pe", tr, reads=[("XS", xb), "IDENT"], writes=[BK(tb)])
                    P.op("dve", lambda e: e.tensor_tensor(
                        HT[:, hb, :, il * 128:(il + 1) * 128],
                        bank_bf16(tb).rearrange("p (k t) -> p k t", t=128),
                        GPREC[:, l, :, None].broadcast_to([128, 8, 128]), ALU.mult),
                        reads=[BK(tb), "GPREC"], writes=[("HT", hb, il)])
                xs_c.append(c_xs); tr_c.append(c_tr)
            return xs_c, tr_c

        def gate_pipeline(b, out_ap, out_key):
            P.op("act", lambda e: e.activation(out_ap, bank_f32(b), AF.Silu),
                 reads=[BK(b)], writes=[out_key])

        def mm_tok(b, hb, il, c0, n):
            def f(e):
                for k in range(8):
                    ins = e.matmul(bank_f32(b)[:, 0:n], lhsT=HT[:, hb, k, il * 128:(il + 1) * 128],
                                   rhs=WIN[:, k, c0:c0 + n], start=(k == 0), stop=(k == 7))
                return ins
            P.op("pe", f, reads=[("HT", hb, il), ("WIN", WIN_KEY[c0])], writes=[BK(b)])

        def mm_feat(b, hb, lhs_fn, wkey):
            def f(e):
                for k in range(8):
                    ins = e.matmul(bank_f32(b), lhsT=lhs_fn(k), rhs=HT[:, hb, k, :], start=(k == 0), stop=(k == 7))
                return ins
            P.op("pe", f, reads=[("HT", hb, 0), ("HT", hb, 1), ("HT", hb, 2), ("HT", hb, 3), ("WIN", wkey)],
                 writes=[BK(b)])

        def inproj_a_chunks(l, G):
            hb = 0
            out = []
            for il in range(4):
                i = 4 * G + il
                slot = (l * NT + i) % NR

                def c_u(il=il, i=i):
                    b = alloc_gen()
                    mm_tok(b, hb, il, C_U, 512)
                    P.op("act", lambda e: e.activation(U[:, (l * NT + i) % NU, :], bank_f32(b), AF.Copy),
                         reads=[BK(b)], writes=[("U", (l * NT + i) % NU)])

                def c_v(il=il, slot=slot):
                    b = alloc_gen()
                    mm_tok(b, hb, il, C_V, 128)
                    P.op("act", lambda e: e.activation(
                        VA[:, slot, :, 0:64], bank_f32(b)[:, 0:128].rearrange("p (k d) -> p k d", d=64), AF.Copy),
                        reads=[BK(b)], writes=[("VA", slot)])
                out += [c_u, c_v]
            slot0 = (l * NT + 4 * G) % NR
            for kv in range(2):
                def c_k(kv=kv):
                    b = alloc_gen()

                    def fk(e):
                        c0 = C_K + kv * 64
                        for k in range(8):
                            e.matmul(bank_f32(b)[0:64, :], lhsT=WIN[:, k, c0:c0 + 64], rhs=HT[:, hb, k, :],
                                     start=(k == 0), stop=(k == 7))
                            ins = e.matmul(bank_f32(b)[64:128, :], lhsT=WIN[:, k, c0:c0 + 64], rhs=HT[:, hb, k, :],
                                           start=(k == 0), stop=(k == 7))
                        return ins
                    P.op("pe", fk, reads=[("HT", hb, 0), ("HT", hb, 1), ("HT", hb, 2), ("HT", hb, 3), ("WIN", WIN_KEY[C_K])],
                         writes=[BK(b)])
                    P.op("act", lambda e: e.activation(
                        KT[:, kv, slot0:slot0 + 4, :], bank_f32(b).rearrange("p (s t) -> p s t", t=128), AF.Copy),
                        reads=[BK(b)], writes=[("KT", kv, slot0 + j) for j in range(4)])
                out.append(c_k)
            return out

        def inproj_b_chunks(l, G):
            hb = 0
            qc, gc = [], []
            for c in range(4):
                def c_q(c=c):
                    b = alloc_gen()
                    mm_feat(b, hb, lambda k: WIN[:, k, C_Q + c * 128:C_Q + (c + 1) * 128], WIN_KEY[C_Q])
                    P.op("act", lambda e: e.activation(QT[:, c, :], bank_f32(b), AF.Identity, scale=0.125),
                         reads=[BK(b)], writes=[("QT", c)])
                qc.append(c_q)
            for il in range(4):
                def c_ag(il=il):
                    b = alloc_gen()
                    mm_tok(b, hb, il, C_AG, 512)
                    gate_pipeline(b, SGA[:, il, :], ("SGA", il))
                gc.append(c_ag)
            for c in range(4):
                def c_pg(c=c):
                    b = alloc_gen()
                    mm_feat(b, hb, lambda k: WIN[:, k, C_PG + c * 128:C_PG + (c + 1) * 128], WIN_KEY[C_PG])
                    gate_pipeline(b, SGP[:, c, :], ("SGP", c))
                gc.append(c_pg)
            return gc[4:] + gc[:4] + qc

        HPAIRS = [(0, 2), (1, 3), (4, 6), (5, 7)]
        HSLOT = {h: 2 * hp + j for hp, pr in enumerate(HPAIRS) for j, h in enumerate(pr)}

        def attn_chunks(l, G, splice=None):
            sc, pv, nrm, aot = {}, {}, {}, {}
            for n in range(4):
                i = 4 * G + n
                il = n
                buf = i % 2
                slot = (l * NT + i) % NR
                pslot = (l * NT + i - 1) % NR
                sc[n] = []
                for hp in range(4):
                    def c_sc(hp=hp, i=i, il=il, buf=buf, slot=slot, pslot=pslot):
                        b = 4 + state["sc"] % 2
                        state["sc"] += 1
                        h0 = HPAIRS[hp][0]

                        def f(e):
                            scv = bank_f32(b).rearrange("p (h c) -> p h c", c=256)
                            for j in range(2):
                                h = HPAIRS[hp][j]
                                kv = h // 4
                                r0 = (h % 2) * 64
                                c = h // 2
                                if i > 0:
                                    e.matmul(scv[:, j, 0:128], lhsT=KT[r0:r0 + 64, kv, pslot, :],
                                             rhs=QT[r0:r0 + 64, c, il * 128:(il + 1) * 128], start=True, stop=True)
                                ins = e.matmul(scv[:, j, 128:256], lhsT=KT[r0:r0 + 64, kv, slot, :],
                                               rhs=QT[r0:r0 + 64, c, il * 128:(il + 1) * 128], start=True, stop=True)
                            return ins
                        rd = [("QT", HPAIRS[hp][0] // 2), ("QT", HPAIRS[hp][1] // 2), ("KT", 0, slot), ("KT", 1, slot)]
                        if i > 0:
                            rd += [("KT", 0, pslot), ("KT", 1, pslot)]
                        P.op("pe", f, reads=rd, writes=[BK(b)])
                        c0 = 0 if i > 0 else 128
                        if i > 0:
                            pt_ap = PT[:, buf, 2 * hp:2 * hp + 2, :].rearrange("p h c -> p (h c)")
                            bi_ap = BIAS[:, 2 * hp:2 * hp + 2, :].rearrange("p h c -> p (h c)")
                            ps_ap = bank_f32(b)
                        else:
                            pt_ap = PT[:, buf, 2 * hp:2 * hp + 2, 128:256]
                            bi_ap = BIAS[:, 2 * hp:2 * hp + 2, 128:256]
                            ps_ap = bank_f32(b).rearrange("p (h c) -> p h c", c=256)[:, :, 128:256]
                        P.op("act", lambda e: e.activation(pt_ap, ps_ap, AF.Exp),
                             reads=[BK(b)], writes=[("PT", buf, hp)])
                        P.op("dve", lambda e: e.tensor_tensor(pt_ap, pt_ap, bi_ap, ALU.mult),
                             reads=[("PT", buf, hp), "BIAS"], writes=[("PT", buf, hp)])
                    sc[n].append(c_sc)

                pv[n] = []
                for half in range(2):
                    def c_pv(half=half, i=i, buf=buf, slot=slot, pslot=pslot):
                        b = 6 + half

                        def f(e):
                            pvv = bank_f32(b)[:, 0:260].rearrange("p (h d) -> p h d", d=65)
                            for j in range(4):
                                h = 4 * half + j
                                kv = half
                                if i > 0:
                                    e.matmul(pvv[:, j, :], lhsT=PT[:, buf, HSLOT[h], 0:128], rhs=VA[:, pslot, kv, :],
                                             start=True, stop=False)
                                ins = e.matmul(pvv[:, j, :], lhsT=PT[:, buf, HSLOT[h], 128:256], rhs=VA[:, slot, kv, :],
                                               start=(i == 0), stop=True)
                            return ins
                        rd = [("PT", buf, 2 * half), ("PT", buf, 2 * half + 1), ("VA", slot), "VA1"]
                        if i > 0:
                            rd.append(("VA", pslot))
                        P.op("pe", f, reads=rd, writes=[BK(b)])
                        P.op("dve", lambda e: e.tensor_copy(PVS[:, buf, half, :], bank_f32(b)[:, 0:260]),
                             reads=[BK(b)], writes=[("PVS", buf, half)])
                        P.op("dve", lambda e: e.tensor_tensor(
                            DEN[:, buf, 4 * half:4 * half + 4],
                            PVS[:, buf, half, :].rearrange("p (h d) -> p h d", d=65)[:, :, 64],
                            ESINK[:, l, 4 * half:4 * half + 4], ALU.add),
                            reads=[("PVS", buf, half), "ESINK"], writes=[("DEN", buf, half)])
                    pv[n].append(c_pv)

                def c_nrm(il=il, buf=buf):
                    for half in range(2):
                        P.op("dve", lambda e, half=half: e.tensor_tensor(
                            T[:, 0, half * 256:(half + 1) * 256].rearrange("p (h d) -> p h d", d=64),
                            PVS[:, buf, half, :].rearrange("p (h d) -> p h d", d=65)[:, :, 0:64],
                            SGA[:, il, half * 256:(half + 1) * 256].rearrange("p (h d) -> p h d", d=64), ALU.mult),
                            reads=[("PVS", buf, half), ("SGA", il)], writes=[("T", 0, 0)])
                    P.op("dve", lambda e: e.reciprocal(RDEN[:, buf, :], DEN[:, buf, :]),
                         reads=[("DEN", buf, 0), ("DEN", buf, 1)], writes=[("RDEN", buf)])
                    P.op("dve", lambda e: e.tensor_tensor(
                        AO[:, 0, :].rearrange("p (h d) -> p h d", d=64),
                        T[:, 0, 0:512].rearrange("p (h d) -> p h d", d=64),
                        RDEN[:, buf, :, None].broadcast_to([128, 8, 64]), ALU.mult),
                        reads=[("T", 0, 0), ("RDEN", buf)], writes=[("AO", 0, 0), ("AO", 0, 1)])
                nrm[n] = c_nrm

                def c_aot(il=il):
                    tb = (0, 3)[state["tp"] % 2]
                    state["tp"] += 1

                    def tr(e):
                        tp = bank_bf16(tb).rearrange("p (k t) -> p k t", t=128)
                        for c in range(4):
                            ins = e.transpose(tp[:, c, :], AO[:, 0, c * 128:(c + 1) * 128], IDENT[:, :])
                        return ins
                    P.op("pe", tr, reads=[("AO", 0, 0), ("AO", 0, 1), "IDENT"], writes=[BK(tb)])
                    P.op("act", lambda e: e.activation(
                        CT[:, 4:8, il * 128:(il + 1) * 128],
                        bank_bf16(tb).rearrange("p (k t) -> p k t", t=128)[:, 0:4, :], AF.Copy),
                        reads=[BK(tb)], writes=[("CTA", il)])
                aot[n] = c_aot
            if splice is None:
                seq = sc[0] + sc[1] + pv[0] + [nrm[0]] + sc[2] + pv[1] + [aot[0], nrm[1]] + sc[3] + pv[2] + \
                    [aot[1], nrm[2]] + pv[3] + [aot[2], nrm[3]]
            else:
                seq = sc[0] + sc[1] + pv[0] + [nrm[0]] + sc[2] + pv[1] + [aot[0], nrm[1]] + splice[0] + sc[3] + \
                    pv[2] + [aot[1], nrm[2]] + splice[1] + pv[3] + [aot[2], nrm[3]] + splice[2]
            return seq, aot[3]

        def pool_chunks(l, G):
            pc, wc = [], []
            for g in range(4):
                pb = g % 2

                def c_p(g=g, pb=pb):
                    b = alloc_gen()

                    def f(e):
                        for il in range(4):
                            i = 4 * G + il
                            o = bank_f32(b)[:, il * 128:(il + 1) * 128]
                            if i == 0:
                                ins = e.matmul(o, lhsT=U[:, (l * NT + i) % NU, g * 128:(g + 1) * 128],
                                               rhs=AT[:, 3 * g + 2, :], start=True, stop=True)
                            else:
                                e.matmul(o, lhsT=U[:, (l * NT + i) % NU, g * 128:(g + 1) * 128],
                                         rhs=AT[:, 3 * g + 0, :], start=True, stop=False)
                                ins = e.matmul(o, lhsT=U[:, (l * NT + i - 1) % NU, g * 128:(g + 1) * 128],
                                               rhs=AT[:, 3 * g + 1, :], start=False, stop=True)
                        return ins
                    rd = ["AT"] + [("U", (l * NT + 4 * G + il) % NU) for il in range(-1 if G > 0 else 0, 4)]
                    P.op("pe", f, reads=rd, writes=[BK(b)])
                    P.op("act", lambda e: e.activation(PLT[:, pb, :], bank_f32(b), AF.Copy),
                         reads=[BK(b)], writes=[("PLT", pb)])

                def c_w(g=g, pb=pb):
                    b2 = alloc_gen()
                    P.op("pe", lambda e: e.matmul(bank_f32(b2), lhsT=PW[:, l, g, :], rhs=PLT[:, pb, :],
                                                  start=True, stop=True),
                         reads=["PW", ("PLT", pb)], writes=[BK(b2)])
                    P.op("dve", lambda e: e.scalar_tensor_tensor(
                        CT[:, g, :], bank_f32(b2), PSC[:, l, g:g + 1], SGP[:, g, :], ALU.mult, ALU.mult),
                        reads=[BK(b2), "PSC", ("SGP", g)], writes=[("CTP", g)])
                pc.append(c_p); wc.append(c_w)
            return [pc[0], pc[1], wc[0], pc[2], wc[1], pc[3], wc[2], wc[3]]

        def out_chunks(l, G):
            out = []
            for il in range(4):
                i = 4 * G + il
                for hf in range(2):
                    def c_y(il=il, i=i, hf=hf):
                        b = alloc_gen()
                        tb = i % 2

                        def f(e):
                            for k in range(8):
                                ins = e.matmul(bank_f32(b), lhsT=CT[:, k, il * 128:(il + 1) * 128],
                                               rhs=WOUT[:, k, hf * 512:(hf + 1) * 512], start=(k == 0), stop=(k == 7))
                            return ins
                        P.op("pe", f, reads=[("CTP", 0), ("CTP", 1), ("CTP", 2), ("CTP", 3), ("CTA", il), ("WOUT", hf)],
                             writes=[BK(b)])
                        P.op("act", lambda e: e.activation(
                            PLT[:, 0, :], bank_f32(b), AF.Square, accum_out=SSQY[:, l, 2 * i + hf:2 * i + hf + 1]),
                            reads=[BK(b)], writes=[("PLT", 0), ("SSQY", l, i, hf), ("BKR", b)])
                        P.op("dve", lambda e: e.tensor_tensor(
                            T[:, tb, hf * 512:(hf + 1) * 512], bank_f32(b), GPOST[:, hf * 512:(hf + 1) * 512], ALU.mult),
                            reads=[BK(b), "GPOST"], writes=[("T", tb, hf), ("BKR", b)])
                    out.append(c_y)

                def c_fin(i=i):
                    tb = i % 2
                    P.op("act", lambda e: e.activation(RSTDY[:, l, i:i + 1], SSQY[:, l, 2 * i:2 * i + 1], AF.Identity,
                                                       bias=SSQY[:, l, 2 * i + 1:2 * i + 2], scale=1.0),
                         reads=[("SSQY", l, i, 0), ("SSQY", l, i, 1)], writes=[("RSTDY", l, i)])
                    P.op("act", lambda e: e.activation(RSTDY[:, l, i:i + 1], RSTDY[:, l, i:i + 1], AF.Ln,
                                                       bias=EPSB[:, :], scale=1.0 / D),
                         reads=[("RSTDY", l, i), "EPSB"], writes=[("RSTDY", l, i)])
                    P.op("act", lambda e: e.activation(RSTDY[:, l, i:i + 1], RSTDY[:, l, i:i + 1], AF.Exp, scale=-0.5),
                         reads=[("RSTDY", l, i)], writes=[("RSTDY", l, i)])
                    P.op("dve", lambda e: e.scalar_tensor_tensor(
                        X[:, i, :], T[:, tb, :], RSTDY[:, l, i:i + 1], X[:, i, :], ALU.mult, ALU.add),
                        reads=[("T", tb, 0), ("T", tb, 1), ("RSTDY", l, i), ("X", i)], writes=[("X", i)])
                    if l == n_layers - 1:
                        P.dma("sp", lambda e: e.dma_start(out=out_t[i], in_=X[:, i, :]), reads=[("X", i)], group="out")
                out.append(c_fin)
            return out

        def interleave(*lists):
            items_ = []
            for li, L in enumerate(lists):
                n = len(L)
                for k, c in enumerate(L):
                    items_.append(((k + 0.5) / n, li, k, c))
            items_.sort(key=lambda t: (t[0], t[1], t[2]))
            return [t[3] for t in items_]

        def run(chunks):
            for c in chunks:
                c()

        items = [(l, G) for l in range(n_layers) for G in range(n_groups)]
        load_weights(0, first=True)
        state["banks"] = [1, 2, 4, 5, 6, 7]
        xs0, tr0 = norm_chunks(0, 0)
        run([xs0[0], xs0[1], tr0[0], xs0[2], tr0[1], xs0[3], tr0[2], tr0[3]])
        load_consts(gate=[("HT", 0, 3)])
        load_wout(0, gate=[("HT", 0, 3)])
        nxg = 1
        pending_tr = None
        ipb0 = inproj_b_chunks(0, 0)
        if len(items) > 1 and n_groups > 1:
            load_x_group(1, gate=[("HT", 0, 3)])
            nxg = 2
            xs1p, pending_tr = norm_chunks(*items[1])
            run(inproj_a_chunks(0, 0) + ipb0[:8] + interleave(ipb0[8:], xs1p))
        else:
            run(inproj_a_chunks(0, 0) + ipb0)
        P.op("act", lambda e: e.activation(ESINK[:, :, :], ESINK[:, :, :], AF.Exp),
             reads=["ESINK"], writes=["ESINK"])
        for n, (l, G) in enumerate(items):
            while nxg < NG and nxg <= n + 2:
                load_x_group(nxg, gate=[("HT", 0, 3)])
                nxg += 1
            if n + 1 < len(items):
                l2, G2 = items[n + 1]
                if l2 != l:
                    load_weights(l2)
                if pending_tr is None:
                    xs1, tr1 = norm_chunks(l2, G2)
                    bstream = [xs1[0], xs1[1], xs1[2], tr1[0], xs1[3], tr1[1], tr1[2], tr1[3]]
                else:
                    bstream = pending_tr
                state["banks"] = [1, 2]
                at_seq, aot_last = attn_chunks(l, G)
                p_sc = 23
                if n > 0:
                    at_seq = at_seq[8:]
                    p_sc -= 8
                extra = []
                pending_tr = None
                if n + 2 < len(items):
                    l3, G3 = items[n + 2]
                    xs3, pending_tr = norm_chunks(l3, G3)
                    extra = xs3
                ipb = inproj_b_chunks(l2, G2)
                bfull = bstream + inproj_a_chunks(l2, G2) + extra
                nb1 = (len(bfull) * p_sc) // len(at_seq)
                run(interleave(at_seq[:p_sc], pool_chunks(l, G), bfull[:nb1]))
                run(interleave(at_seq[p_sc:], bfull[nb1:], ipb[8:]))
                state["banks"] = [1, 2, 6, 7]
                oc = out_chunks(l, G)
                oc_seq = oc[:5] + [aot_last] + oc[5:]
                nhead = attn_chunks(l2, G2)[0][:8]
                run(oc_seq[:6] + interleave(oc_seq[6:], nhead))
                state["banks"] = [1, 2, 4, 5, 6, 7]
                run(ipb[:8])
                if l2 != l:
                    load_wout(l2)
            else:
                state["banks"] = [1, 2]
                oc = out_chunks(l, G)
                at_seq, aot_last = attn_chunks(l, G, splice=[oc[0:3], oc[3:6], oc[6:9]])
                pcs = pool_chunks(l, G)
                if n > 0:
                    at_seq = at_seq[8:]
                    run(interleave(at_seq[:4], pcs) + at_seq[4:])
                else:
                    run(interleave(at_seq[:10], pcs) + at_seq[10:])
                state["banks"] = [1, 2, 4, 5, 6, 7]
                run([aot_last] + oc[9:])
        fw = ["out"]
        if dbg:
            dbg_list = [("U", U, [128, NU, 512]), ("QT", QT, [128, 4, 512]), ("KT", KT, [128, 2, NR, 128]),
                        ("VA", VA, [128, NR, 2, 65]), ("SGA", SGA, [128, 4, 512]), ("SGP", SGP, [128, 4, 512]),
                        ("CT", CT, [128, 8, 512]), ("HT", HT, [128, 1, 8, 512]), ("X", X, [128, NT, D]),
                        ("T", T, [128, 2, D]), ("RSTD", RSTD, [128, DEPTH, NT]), ("PT", PT, [128, 2, 8, 256]),
                        ("AO", AO, [128, 1, 512]), ("RDEN", RDEN, [128, 2, 8]), ("PLT", PLT, [128, 2, 512]),
                        ("RSTDY", RSTDY, [128, DEPTH, NT]), ("ESINK", ESINK, [128, DEPTH, 8]),
                        ("BIAS", BIAS, [128, 8, 256]), ("AT", AT, [128, 12, 128]), ("IDENT", IDENT, [128, 128])]
            allkeys = set()
            for o in P.ops:
                allkeys.update(o.writes)
            for name, tns, shape in dbg_list:
                dd = nc.dram_tensor("dbg_" + name, shape, F32, kind="ExternalOutput").ap()
                full = tuple(slice(None) for _ in shape)
                P.dma("pool", lambda e, dd=dd, tns=tns, full=full: e.dma_start(out=dd, in_=tns[full]),
                      reads=list(allkeys), group="dbg")
            fw.append("dbg")
        P.emit(final_wait_groups=fw)
    return nc


_CACHE = {}


def kernel(x, w_in, pool_w, pool_scale, attn_sinks, w_out, norm_pre, norm_post):
    if "nc" not in _CACHE:
        _CACHE["nc"] = build_program()
        _CACHE["tables"] = _const_tables()
    nc = _CACHE["nc"]
    bias, at = _CACHE["tables"]
    f = lambda a: np.ascontiguousarray(np.asarray(a, dtype=np.float32))
    shared = {"w_in": f(w_in), "pool_w": f(pool_w), "pool_scale": f(pool_scale), "attn_sinks": f(attn_sinks),
              "w_out": f(w_out), "norm_pre": f(norm_pre), "norm_post": f(norm_post), "c_bias": bias, "c_at": at}
    x = f(x)
    in_maps = [dict(shared, x=x[b]) for b in range(8)]
    res = run_bass_kernel_spmd(nc, in_maps, core_ids=list(range(8)))
    return np.stack([np.asarray(r["out"]) for r in res.results], axis=0).astype(np.float32)
```
